# Optimizing a Trainium2 kernel written in Bass

```python
import math
import jax, jax.numpy as jnp
from jax import lax
import numpy as np

D_MODEL = 1024
BATCH = 4
SEQ = 8192
DEPTH = 1

GRID_W = 64
CTX_LEN = 256
MIX_DIM = D_MODEL
ATT_DIM = MIX_DIM // 2
CONV_DIM = MIX_DIM - ATT_DIM
HEAD_DIM = 64
N_HEADS = ATT_DIM // HEAD_DIM
NA_ROWS = 8
NA_COLS = 16
CONV_WIDTH = 31
D_FF = 4 * D_MODEL
IN_PROJ_DIM = 3 * ATT_DIM + 2 * CONV_DIM
N_MOD = 6
ROPE_BASE = 10000.0
EPS = 1e-6

kernel_name = 'hymba_na_conformer_dit_block'


def rms_norm(x, g):
    xf = x.astype(jnp.float32)
    y = xf * lax.rsqrt(jnp.mean(xf * xf, axis=-1, keepdims=True) + EPS)
    return (y * g.astype(jnp.float32)).astype(x.dtype)


def modulate(h, shift, scale):
    return h * (1 + scale) + shift


def split_heads(t):
    b, l, _ = t.shape
    return t.reshape(b, l, N_HEADS, HEAD_DIM)


def split_proj(p):
    q = split_heads(p[..., :ATT_DIM])
    k = split_heads(p[..., ATT_DIM:2 * ATT_DIM])
    v = split_heads(p[..., 2 * ATT_DIM:3 * ATT_DIM])
    u = p[..., 3 * ATT_DIM:]
    return q, k, v, u


def rope_half(xh, pos):
    n_pairs = xh.shape[-1] // 2
    inv_freq = ROPE_BASE ** (-jnp.arange(n_pairs, dtype=jnp.float32) / n_pairs)
    ang = pos.astype(jnp.float32)[:, None] * inv_freq[None, :]
    cos = jnp.cos(ang)[None, :, None, :]
    sin = jnp.sin(ang)[None, :, None, :]
    x1 = xh[..., :n_pairs].astype(jnp.float32)
    x2 = xh[..., n_pairs:].astype(jnp.float32)
    return jnp.concatenate([x1 * cos - x2 * sin, x2 * cos + x1 * sin], axis=-1).astype(xh.dtype)


def axial_rope(x, row_pos, col_pos):
    half = HEAD_DIM // 2
    return jnp.concatenate([rope_half(x[..., :half], row_pos),
                            rope_half(x[..., half:], col_pos)], axis=-1)


def conformer_conv(u, kernel, bias, ln_g, ln_b):
    a, b = jnp.split(u, 2, axis=-1)
    h = a * jax.nn.sigmoid(b)
    pad = CONV_WIDTH // 2
    h = lax.conv_general_dilated(h, kernel[:, None, :].astype(h.dtype), window_strides=(1,),
                                 padding=[(pad, pad)], dimension_numbers=('NWC', 'WIO', 'NWC'),
                                 feature_group_count=CONV_DIM) + bias
    hf = h.astype(jnp.float32)
    mu = jnp.mean(hf, axis=-1, keepdims=True)
    var = jnp.mean(jnp.square(hf - mu), axis=-1, keepdims=True)
    hn = (hf - mu) * lax.rsqrt(var + EPS) * ln_g + ln_b
    return jax.nn.silu(hn).astype(u.dtype)


def sq_relu_mlp(h, w1, w2):
    return jnp.square(jax.nn.relu(h @ w1)) @ w2


def context_attention(q, k, v):
    b, lc = q.shape[0], q.shape[1]
    s = jnp.einsum('bqhd,bkhd->bhqk', q, k).astype(jnp.float32) * HEAD_DIM ** -0.5
    p = jax.nn.softmax(s, axis=-1).astype(v.dtype)
    o = jnp.einsum('bhqk,bkhd->bqhd', p, v)
    return o.reshape(b, lc, ATT_DIM)


def neighbourhood_attention(q_rot, q, k_rot, v, k_ctx, v_ctx, rpb):
    b, l, h, dh = q.shape
    rows = l // GRID_W
    kr = min(NA_ROWS, rows)
    scale = dh ** -0.5
    q_rot_g = q_rot.reshape(b, rows, GRID_W, h, dh).transpose(1, 0, 2, 3, 4)
    q_g = q.reshape(b, rows, GRID_W, h, dh).transpose(1, 0, 2, 3, 4)
    k_g = k_rot.reshape(b, rows, GRID_W, h, dh)
    v_g = v.reshape(b, rows, GRID_W, h, dh)
    row_start = np.clip(np.arange(rows) - kr // 2, 0, rows - kr).astype(np.int32)
    col_start = np.clip(np.arange(GRID_W) - NA_COLS // 2, 0, GRID_W - NA_COLS)
    col_idx = (col_start[:, None] + np.arange(NA_COLS)[None, :]).astype(np.int32)
    col_off = (col_idx - np.arange(GRID_W)[:, None] + (NA_COLS - 1)).astype(np.int32)
    rpb_cols = rpb[:, :, col_off]

    def one_row(args):
        qr_rot, qr, r, rs = args
        k_rows = lax.dynamic_slice_in_dim(k_g, rs, kr, axis=1)
        v_rows = lax.dynamic_slice_in_dim(v_g, rs, kr, axis=1)
        kw = k_rows[:, :, col_idx]
        vw = v_rows[:, :, col_idx]
        row_off = rs + jnp.arange(kr) - r + (NA_ROWS - 1)
        bias = rpb_cols[:, row_off].transpose(0, 2, 1, 3)
        s_win = jnp.einsum('bqhd,biqjhd->bhqij', qr_rot, kw).astype(jnp.float32) * scale
        s_win = (s_win + bias.astype(jnp.float32)[None]).reshape(b, h, GRID_W, kr * NA_COLS)
        s_ctx = jnp.einsum('bqhd,bchd->bhqc', qr, k_ctx).astype(jnp.float32) * scale
        p = jax.nn.softmax(jnp.concatenate([s_win, s_ctx], axis=-1), axis=-1).astype(v.dtype)
        p_win = p[..., :kr * NA_COLS].reshape(b, h, GRID_W, kr, NA_COLS)
        p_ctx = p[..., kr * NA_COLS:]
        return (jnp.einsum('bhqij,biqjhd->bqhd', p_win, vw)
                + jnp.einsum('bhqc,bchd->bqhd', p_ctx, v_ctx))

    out = lax.map(one_row, (q_rot_g, q_g, jnp.arange(rows, dtype=jnp.int32), jnp.asarray(row_start)))
    return out.transpose(1, 0, 2, 3, 4).reshape(b, l, h * dh)


def setup_inputs(seed: int = 0) -> dict:
    key = jax.random.key(seed)
    ks = jax.random.split(key, 18)
    n = jax.random.normal
    return {
        'x': n(ks[0], (BATCH, SEQ, D_MODEL), jnp.float32),
        'c': n(ks[1], (BATCH, D_MODEL), jnp.float32),
        'ctx': n(ks[2], (BATCH, CTX_LEN, D_MODEL), jnp.float32),
        'c_ctx': n(ks[3], (D_MODEL,), jnp.float32),
        'w_ada': n(ks[4], (DEPTH, D_MODEL, N_MOD * D_MODEL), jnp.float32) * (0.5 * D_MODEL ** -0.5),
        'b_ada': 0.02 * n(ks[5], (DEPTH, N_MOD * D_MODEL), jnp.float32),
        'norm1_g': 1.0 + 0.02 * n(ks[6], (DEPTH, D_MODEL), jnp.float32),
        'norm2_g': 1.0 + 0.02 * n(ks[7], (DEPTH, D_MODEL), jnp.float32),
        'w_in': n(ks[8], (DEPTH, D_MODEL, IN_PROJ_DIM), jnp.float32) * D_MODEL ** -0.5,
        'dw_kernel': n(ks[9], (DEPTH, CONV_WIDTH, CONV_DIM), jnp.float32) * CONV_WIDTH ** -0.5,
        'dw_bias': 0.02 * n(ks[10], (DEPTH, CONV_DIM), jnp.float32),
        'conv_ln_g': 1.0 + 0.02 * n(ks[11], (DEPTH, CONV_DIM), jnp.float32),
        'conv_ln_b': 0.02 * n(ks[12], (DEPTH, CONV_DIM), jnp.float32),
        'rpb': 0.5 * n(ks[13], (DEPTH, N_HEADS, 2 * NA_ROWS - 1, 2 * NA_COLS - 1), jnp.float32),
        'w_out': n(ks[14], (DEPTH, MIX_DIM, D_MODEL), jnp.float32) * MIX_DIM ** -0.5,
        'w_mlp1': n(ks[15], (DEPTH, D_MODEL, D_FF), jnp.float32) * D_MODEL ** -0.5,
        'w_mlp2': n(ks[16], (DEPTH, D_FF, D_MODEL), jnp.float32) * D_FF ** -0.5,
        'final_g': 1.0 + 0.02 * n(ks[17], (D_MODEL,), jnp.float32),
    }


def reference(x, c, ctx, c_ctx, w_ada, b_ada, norm1_g, norm2_g, w_in, dw_kernel, dw_bias,
              conv_ln_g, conv_ln_b, rpb, w_out, w_mlp1, w_mlp2, final_g):
    l = x.shape[1]
    t = jnp.arange(l, dtype=jnp.int32)
    row_pos = t // GRID_W
    col_pos = t % GRID_W
    silu_c = jax.nn.silu(c)
    silu_cc = jax.nn.silu(c_ctx)[None]
    for i in range(DEPTH):
        last = i == DEPTH - 1
        mod_x = (silu_c @ w_ada[i] + b_ada[i])[:, None, :]
        mod_c = (silu_cc @ w_ada[i] + b_ada[i])[:, None, :]
        sh1, sc1, g1, sh2, sc2, g2 = jnp.split(mod_x, N_MOD, axis=-1)
        csh1, csc1, cg1, csh2, csc2, cg2 = jnp.split(mod_c, N_MOD, axis=-1)

        hx = modulate(rms_norm(x, norm1_g[i]), sh1, sc1)
        hc = modulate(rms_norm(ctx, norm1_g[i]), csh1, csc1)
        qx, kx, vx, ux = split_proj(hx @ w_in[i])
        if last:
            kvc = hc @ w_in[i][:, ATT_DIM:3 * ATT_DIM]
            kc, vc = split_heads(kvc[..., :ATT_DIM]), split_heads(kvc[..., ATT_DIM:])
        else:
            qc, kc, vc, uc = split_proj(hc @ w_in[i])

        qx_rot = axial_rope(qx, row_pos, col_pos)
        kx_rot = axial_rope(kx, row_pos, col_pos)
        attn_x = neighbourhood_attention(qx_rot, qx, kx_rot, vx, kc, vc, rpb[i])
        conv_x = conformer_conv(ux, dw_kernel[i], dw_bias[i], conv_ln_g[i], conv_ln_b[i])
        x_new = x + g1 * (jnp.concatenate([attn_x, conv_x], axis=-1) @ w_out[i])

        hx2 = modulate(rms_norm(x_new, norm2_g[i]), sh2, sc2)
        x_new = x_new + g2 * sq_relu_mlp(hx2, w_mlp1[i], w_mlp2[i])

        if not last:
            attn_c = context_attention(qc, kc, vc)
            conv_c = conformer_conv(uc, dw_kernel[i], dw_bias[i], conv_ln_g[i], conv_ln_b[i])
            ctx = ctx + cg1 * (jnp.concatenate([attn_c, conv_c], axis=-1) @ w_out[i])
            hc2 = modulate(rms_norm(ctx, norm2_g[i]), csh2, csc2)
            ctx = ctx + cg2 * sq_relu_mlp(hc2, w_mlp1[i], w_mlp2[i])
        x = x_new
    return rms_norm(x, final_g)
```

```python
import contextlib
import os
import numpy as np
import concourse.bass as bass
import concourse.mybir as mybir
from concourse.bass_utils import run_bass_kernel_spmd

F32 = mybir.dt.float32
BF16 = mybir.dt.bfloat16
AF = mybir.ActivationFunctionType
ALU = mybir.AluOpType

D = 1024
SEQ = 8192
NB = 4
TOK = 4096
HALO = 256
TE = TOK + 2 * HALO
NCH = 9
CTX = 256
EPS = 1e-6
NEG = -30000.0
DEBUG = bool(int(os.environ.get("KDEBUG", "0")))
KSTOP = os.environ.get("KSTOP", "")
KSUB = int(os.environ.get("KSUB", "9"))


class Buf:
    def __init__(self, name=""):
        self.name = name
        self.ws = {}
        self.rs = {}


class Prog:
    def __init__(self, nc):
        self.nc = nc
        self.engs = {"pe": nc.tensor, "act": nc.scalar, "dve": nc.vector, "pool": nc.gpsimd, "sp": nc.sync}
        self.sems = {k: nc.alloc_semaphore("sem_" + k) for k in self.engs}
        self.cnt = {k: 0 for k in self.engs}
        self.waited = {k: {} for k in self.engs}
        self.dma_sems = {}

    def _wait(self, e, deps):
        for key, (sem, val) in deps.items():
            if self.waited[e].get(key, 0) < val:
                self.engs[e].wait_ge(sem, val)
                self.waited[e][key] = val

    def _deps(self, reads, writes):
        deps = {}
        for b in reads:
            for k, v in b.ws.items():
                if k not in deps or deps[k][1] < v[1]:
                    deps[k] = v
        for b in writes:
            for k, v in b.rs.items():
                if k not in deps or deps[k][1] < v[1]:
                    deps[k] = v
        return deps

    def _mark(self, key, tok, reads, writes):
        for b in writes:
            if b.rs:
                b.ws = {}
                b.rs = {}
            b.ws[key] = tok
        for b in reads:
            b.rs[key] = tok

    def op(self, e, fn, reads=(), writes=()):
        self._wait(e, self._deps(reads, writes))
        inst = fn(self.engs[e])
        self.cnt[e] += 1
        inst.then_inc(self.sems[e], 1)
        self._mark(e, (self.sems[e], self.cnt[e]), reads, writes)

    def mm(self, mms, reads=(), writes=()):
        self._wait("pe", self._deps(reads, writes))
        inst = None
        for fn in mms:
            inst = fn(self.engs["pe"])
        self.cnt["pe"] += 1
        inst.then_inc(self.sems["pe"], 1)
        self._mark("pe", (self.sems["pe"], self.cnt["pe"]), reads, writes)

    def dma(self, e, key, out, in_, reads=(), writes=()):
        if key in ("c0", "c1"):
            self.nuniq = getattr(self, "nuniq", 0) + 1
            key = "u%d" % self.nuniq
        self._wait(e, self._deps(reads, writes))
        if key not in self.dma_sems:
            self.dma_sems[key] = [self.nc.alloc_semaphore("sd_" + key), 0]
        s = self.dma_sems[key]
        s[1] += 16
        self.engs[e].dma_start(out=out, in_=in_).then_inc(s[0], 16)
        self._mark("dma_" + key, (s[0], s[1]), reads, writes)

    def barrier(self):
        deps = {k: (self.sems[k], self.cnt[k]) for k in self.engs if self.cnt[k] > 0}
        for key, s in self.dma_sems.items():
            deps["dma_" + key] = (s[0], s[1])
        for e in self.engs:
            self._wait(e, deps)


class Banks:
    def __init__(self, nc):
        self.t = [nc.alloc_psum_tensor("bank%d" % i, [128, 512], F32) for i in range(8)]
        self.buf = [Buf("bank%d" % i) for i in range(8)]
        self.free = list(range(8))

    def get(self):
        assert self.free, "out of PSUM banks"
        i = self.free.pop(0)
        return i

    def put(self, i):
        self.free.append(i)


def build_nc():
    nc = bass.Bass("TRN2", target_bir_lowering=False)
    P = Prog(nc)
    BK = Banks(nc)

    def din(name, shape, dt=F32):
        return nc.dram_tensor(name, list(shape), dt, kind="ExternalInput").ap()

    xe = din("xe", [TE, D])
    ctxb = din("ctxb", [CTX, D])
    cvec = din("cvec", [128, 8, 2])
    w_ada = din("w_ada", [D, 6 * D])
    b_adaT = din("b_adaT", [128, 48])
    b_ada_g = din("b_ada_g", [1, 2 * D])
    n1gT = din("n1gT", [128, 8])
    n2gT = din("n2gT", [128, 8])
    fgB_d = din("fgB", [128, D])
    w_in = din("w_in", [D, 2560])
    w_out = din("w_out", [D, D])
    w1 = din("w1", [D, 4 * D])
    w2 = din("w2", [4 * D, D])
    dwk = din("dwk", [128, 4, 31])
    cvp = din("cvp", [128, 3, 4])
    ident_d = din("ident", [128, 128])
    perm_d = din("perm", [128, 128])
    rope_d = din("rope", [NCH, 128, 2, 512])
    mtab_d = din("mtab", [128, 8, 1536])
    khot_d = din("khot", [12, 8, 1536])
    bmask_d = din("bmask", [NCH, 12, 8, 512])
    vm_d = din("vm", [2, 128, 512])
    out_d = nc.dram_tensor("out", [TOK, D], F32, kind="ExternalOutput").ap()
    xnew_d = nc.dram_tensor("xnew", [TOK, D], F32).ap()
    dbg = {}
    if DEBUG:
        dbg["attnT"] = nc.dram_tensor("dbg_attnT", [128, 4, TOK], F32, kind="ExternalOutput").ap()
        dbg["modT"] = nc.dram_tensor("dbg_modT", [128, 64], F32, kind="ExternalOutput").ap()
        dbg["g1B"] = nc.dram_tensor("dbg_g1B", [128, D], F32, kind="ExternalOutput").ap()
        dbg["hx0"] = nc.dram_tensor("dbg_hx0", [128, 8, 512], F32, kind="ExternalOutput").ap()
        dbg["xnew"] = nc.dram_tensor("dbg_xnew", [TOK, D], F32, kind="ExternalOutput").ap()
        dbg["convT"] = nc.dram_tensor("dbg_convT", [128, 4, 512], F32, kind="ExternalOutput").ap()
        dbg["q0"] = nc.dram_tensor("dbg_q0", [128, 8, 512], F32, kind="ExternalOutput").ap()
        dbg["mstat"] = nc.dram_tensor("dbg_mstat", [128, 4, 512], F32, kind="ExternalOutput").ap()
        dbg["ysq"] = nc.dram_tensor("dbg_ysq", [128, 4, 512], F32, kind="ExternalOutput").ap()
        dbg["hring"] = nc.dram_tensor("dbg_hring", [128, 4, 1808], BF16, kind="ExternalOutput").ap()

    ES = contextlib.ExitStack()

    def sb(stack, name, shape, dt):
        return stack.enter_context(nc.sbuf_tensor("s_" + name, list(shape), dt))

    identf = sb(ES, "identf", [128, 128], F32)
    onesf = sb(ES, "onesf", [128, 128], F32)
    onesb = sb(ES, "onesb", [128, 128], BF16)
    permb = sb(ES, "permb", [128, 128], BF16)
    modT = sb(ES, "modT", [128, 32, 2], F32)
    gsT = sb(ES, "gsT", [128, 3, 8], F32)
    n1g = sb(ES, "n1g", [128, 8], F32)
    n2g = sb(ES, "n2g", [128, 8], F32)
    stat = sb(ES, "stat", [128, 16], F32)
    junk = sb(ES, "junk", [128, D], BF16)
    gB_d = nc.dram_tensor("gB_scr", [2, 128, D], F32).ap()
    b_const = Buf("const")
    b_mod = Buf("mod")
    b_stat = Buf("stat")
    b_junk = Buf("junk")

    P.dma("sp", "c0", identf[:], ident_d, writes=[b_const])
    P.dma("pool", "c1", permb[:], perm_d, writes=[b_const])
    P.dma("sp", "c0", n1g[:], n1gT, writes=[b_const])
    P.dma("sp", "c0", n2g[:], n2gT, writes=[b_const])
    P.op("dve", lambda e: e.memset(onesf[:], 1.0), writes=[b_const])
    P.op("dve", lambda e: e.memset(onesb[:], 1.0), writes=[b_const])

    def evac(i, fn_engine, fn, reads=(), writes=()):
        P.op(fn_engine, fn, reads=[BK.buf[i]] + list(reads), writes=writes)

    A = contextlib.ExitStack()
    attnT = sb(A, "attnT", [128, 4, TOK], BF16)
    xt = sb(A, "xt", [128, 4, D], F32)
    b_xt = Buf("xt")
    hxT = sb(A, "hxT", [128, 8, 512], BF16)
    b_hx = Buf("hx")
    b_attn = [Buf("attn%d" % i) for i in range(16)]
    win_v = w_in.rearrange("(kt p) n -> p kt n", p=128)
    xrows = lambda j: (lambda t: xe[j * 512 + t * 128:j * 512 + (t + 1) * 128, :])

    W = contextlib.ExitStack()
    winA = sb(W, "winA", [128, 8, 1536], BF16)
    b_w = Buf("winA")
    mtab = sb(W, "mtab", [128, 8, 1536], BF16)
    b_mtab = Buf("mtab")
    mt_src = mtab_d.rearrange("p h n -> p (h n)")
    mt_dst = mtab[:].rearrange("p h n -> p (h n)")
    xt_flat = xt[:].rearrange("p t n -> p (t n)")
    for r3 in range(3):
        P.dma("sp", "xl", xt_flat, mt_src[:, r3 * 4096:(r3 + 1) * 4096], writes=[b_xt])
        P.op("act", lambda e, r3=r3: e.activation(mt_dst[:, r3 * 4096:(r3 + 1) * 4096], xt_flat, AF.Exp), reads=[b_xt], writes=[b_mtab])

    with contextlib.ExitStack() as S:
        cv = sb(S, "cv", [128, 8, 2], F32)
        scT = sb(S, "scT", [128, 8, 2], BF16)
        scTf = sb(S, "scTf", [128, 8, 2], F32)
        scB = sb(S, "scB", [128, 8, 128], BF16)
        badaT = sb(S, "badaT", [128, 48], F32)
        badag = sb(S, "badag", [1, 2 * D], BF16)
        gB = sb(S, "gB", [128, 2, D], F32)
        slab = [sb(S, "slab%d" % i, [128, 8, D], BF16) for i in range(2)]
        b_slab = [Buf("slab0"), Buf("slab1")]
        b_s = Buf("s_misc")
        b_gB = Buf("gB")
        P.dma("sp", "c0", cv[:], cvec, writes=[b_s])
        P.dma("sp", "c0", badaT[:], b_adaT, writes=[b_s])
        P.dma("pool", "c1", badag[:], b_ada_g, writes=[b_s])
        wada_v = w_ada.rearrange("(kt p) n -> p kt n", p=128)
        order = [0, 1, 3, 4, 2, 5]
        P.dma("pool", "slab0", slab[0][:], wada_v[:, :, order[0] * D:(order[0] + 1) * D], writes=[b_slab[0]])
        P.dma("pool", "slab1", slab[1][:], wada_v[:, :, order[1] * D:(order[1] + 1) * D], writes=[b_slab[1]])
        PREFETCH_WIN = True
        P.op("act", lambda e: e.activation(scTf[:], cv[:], AF.Silu), reads=[b_s], writes=[b_s])
        P.op("act", lambda e: e.activation(scT[:], cv[:], AF.Silu), reads=[b_s], writes=[b_s])
        for kt in range(8):
            P.op("dve", lambda e, kt=kt: e.tensor_scalar(scB[:, kt, :], onesb[:], scTf[:, kt, 0:1], None, ALU.mult),
                 reads=[b_s, b_const], writes=[b_s])
        bi = BK.get()
        for n, v in enumerate(order):
            s = n % 2
            if v in (0, 1, 3, 4):
                vi = {0: 0, 1: 1, 3: 2, 4: 3}[v]
                mms = []
                for mt in range(8):
                    for kt in range(8):
                        mms.append(lambda e, mt=mt, kt=kt, vi=vi, s=s: e.matmul(
                            BK.t[bi][:, (vi * 8 + mt) * 2:(vi * 8 + mt) * 2 + 2], slab[s][:, kt, mt * 128:(mt + 1) * 128],
                            scT[:, kt, :], start=(kt == 0), stop=(kt == 7)))
                P.mm(mms, reads=[b_slab[s], b_s], writes=[BK.buf[bi]])
            else:
                gi = 0 if v == 2 else 1
                for nh in range(2):
                    bj = BK.get()
                    mms = []
                    for kt in range(8):
                        mms.append(lambda e, kt=kt, s=s, nh=nh, bj=bj: e.matmul(
                            BK.t[bj][:], scB[:, kt, :], slab[s][:, kt, nh * 512:(nh + 1) * 512], start=(kt == 0), stop=False))
                    mms.append(lambda e, nh=nh, bj=bj, gi=gi: e.matmul(
                        BK.t[bj][:], onesb[0:1, :], badag[0:1, gi * D + nh * 512:gi * D + (nh + 1) * 512], start=False, stop=True))
                    P.mm(mms, reads=[b_slab[s], b_s, b_const], writes=[BK.buf[bj]])
                    evac(bj, "act", lambda e, bj=bj, nh=nh, gi=gi: e.activation(gB[:, gi, nh * 512:(nh + 1) * 512], BK.t[bj][:], AF.Copy),
                         writes=[b_gB])
                    BK.put(bj)
            if n + 2 < 6:
                v2 = order[n + 2]
                P.dma("pool", "slab%d" % s, slab[s][:], wada_v[:, :, v2 * D:(v2 + 1) * D], writes=[b_slab[s]])
        for kt in range(0, 8, 2):
            P.dma("pool", "win", winA[:, kt:kt + 2, :], win_v[:, kt:kt + 2, 0:1536], writes=[b_w])
        for vi, v in enumerate([0, 1, 3, 4]):
            for w in range(2):
                evac(bi, "dve", lambda e, vi=vi, v=v, w=w: e.tensor_tensor(
                    modT[:, vi * 8:(vi + 1) * 8, w],
                    BK.t[bi][:, vi * 16:(vi + 1) * 16].rearrange("p (m w) -> p m w", w=2)[:, :, w],
                    badaT[:, v * 8:(v + 1) * 8], ALU.add), reads=[b_s], writes=[b_mod])
        BK.put(bi)
        for i, (gsrc, vi, w) in enumerate([(n1g, 1, 0), (n1g, 1, 1), (n2g, 3, 0)]):
            P.op("dve", lambda e, i=i, gsrc=gsrc, vi=vi, w=w: e.scalar_tensor_tensor(
                gsT[:, i, :], modT[:, vi * 8:(vi + 1) * 8, w], 1.0, gsrc[:], ALU.add, ALU.mult),
                reads=[b_mod, b_const], writes=[b_mod])
        P.dma("sp", "c0", gB_d.rearrange("g p n -> p g n"), gB[:], reads=[b_gB])
        if DEBUG:
            P.dma("sp", "dbg", dbg["modT"], modT[:].rearrange("p a b -> p (a b)"), reads=[b_mod])
            P.dma("sp", "dbg", dbg["g1B"], gB[:, 0, :], reads=[b_gB])
        P.barrier()

    if KSTOP == "S":
        ES.close()
        return nc
    SH1 = lambda kt, w: modT[:, 0 * 8 + kt, w:w + 1]
    SH2 = lambda kt: modT[:, 2 * 8 + kt, 0:1]

    def norm_dma(xt, b_xt, src_rows, ntile, dmakey):
        for t in range(ntile):
            P.dma("sp", dmakey, xt[:, t, :], src_rows(t), writes=[b_xt])

    def norm_stats_units(xt, b_xt, ntile):
        for t in range(ntile):
            P.op("act", lambda e, t=t: e.activation(junk[:], xt[:, t, :], AF.Square, accum_out=stat[:, t:t + 1]),
                 reads=[b_xt, b_junk], writes=[b_stat, b_junk])
            yield
        P.op("act", lambda e: e.activation(stat[:, 4:4 + ntile], stat[:, 0:ntile], AF.Ln, scale=1.0 / D, bias=EPS),
             reads=[b_stat], writes=[b_stat])
        P.op("act", lambda e: e.activation(stat[:, 8:8 + ntile], stat[:, 4:4 + ntile], AF.Exp, scale=-0.5), reads=[b_stat], writes=[b_stat])
        for t in range(ntile):
            P.op("pool", lambda e, t=t: e.tensor_scalar(xt[:, t, :], xt[:, t, :], stat[:, 8 + t:9 + t], 1.0, ALU.mult, ALU.mult),
                 reads=[b_stat, b_xt], writes=[b_xt])
        yield

    def norm_stats(xt, b_xt, ntile):
        for _ in norm_stats_units(xt, b_xt, ntile):
            pass

    def norm_tp(xt, b_xt, hxT, b_hx, ntile, gm_idx, sh_fn, kts=range(8)):
        ncols = 128 * ntile
        for kt in kts:
            bi = BK.get()
            P.mm([lambda e, t=t, kt=kt, bi=bi: e.transpose(BK.t[bi][:, t * 128:(t + 1) * 128], xt[:, t, kt * 128:(kt + 1) * 128], identf[:])
                  for t in range(ntile)], reads=[b_xt, b_const], writes=[BK.buf[bi]])
            if kt % 2 == 0:
                evac(bi, "act", lambda e, kt=kt, bi=bi: e.activation(hxT[:, kt, 0:ncols], BK.t[bi][:, 0:ncols], AF.Identity,
                                                                      scale=gsT[:, gm_idx, kt:kt + 1], bias=sh_fn(kt)),
                     reads=[b_mod], writes=[b_hx])
            else:
                evac(bi, "dve", lambda e, kt=kt, bi=bi: e.tensor_scalar(hxT[:, kt, 0:ncols], BK.t[bi][:, 0:ncols],
                                                                         gsT[:, gm_idx, kt:kt + 1], sh_fn(kt), ALU.mult, ALU.add),
                     reads=[b_mod], writes=[b_hx])
            BK.put(bi)

    def proj_fm(hxT, b_hx, wt, b_wt, col0, ncols, bi):
        P.mm([lambda e, kt=kt: e.matmul(BK.t[bi][:, 0:ncols], wt[:, kt, col0:col0 + 128], hxT[:, kt, 0:ncols],
                                        start=(kt == 0), stop=(kt == 7)) for kt in range(8)],
             reads=[b_hx, b_wt], writes=[BK.buf[bi]])

    with contextlib.ExitStack() as A1:
        kT = sb(A1, "kT", [128, 8, 3, 512], BF16)
        qrT = sb(A1, "qrT", [128, 8, 2, 512], BF16)
        qun = sb(A1, "qun", [128, 4, 2, 512], BF16)
        kun = [sb(A1, "kun%d" % i, [128, 512], BF16) for i in range(2)]
        vaug = sb(A1, "vaug", [128, 12, 4, 3, 64], BF16)
        kcT = sb(A1, "kcT", [128, 8, CTX], BF16)
        vcaug = sb(A1, "vcaug", [128, 2, 4, 3, 64], BF16)
        ropet = sb(A1, "ropet", [128, 2, 512], F32)
        t1 = [sb(A1, "t1_%d" % i, [128, 512], F32) for i in range(2)]
        t2 = [sb(A1, "t2_%d" % i, [128, 512], F32) for i in range(2)]
        NE = 6
        Eb = [sb(A1, "Eb%d" % i, [128, 512], BF16) for i in range(NE)]
        Pb = Eb
        rcp = [sb(A1, "rcp%d" % i, [128, 256], F32) for i in range(2)]
        b_kslot = [Buf("k0"), Buf("k1"), Buf("k2")]
        b_qslot = [Buf("q0"), Buf("q1")]
        b_qun = [[Buf("qun%d_%d" % (i, m)) for m in range(4)] for i in range(2)]
        b_vslot = [Buf("v0"), Buf("v1"), Buf("v2")]
        b_kun = [Buf("kun0"), Buf("kun1")]
        b_rope = Buf("rope")
        b_t1 = [Buf("t1a"), Buf("t1b")]
        b_t2 = [Buf("t2a"), Buf("t2b")]
        b_E = [Buf("E%d" % i) for i in range(NE)]
        b_P = b_E
        b_rcp = [Buf("rcp0"), Buf("rcp1")]
        b_ctx = Buf("ctxkv")

        P.op("pool", lambda e: e.memset(vaug[:].rearrange("p a b c d -> p (a b) c d")[:, :, 1, :], 1.0), writes=b_vslot)
        P.op("pool", lambda e: e.memset(vcaug[:].rearrange("p a b c d -> p (a b) c d")[:, :, 1, :], 1.0), writes=[b_ctx])
        P.op("pool", lambda e: e.memset(kcT[:], 0.0), writes=[b_ctx])
        P.dma("pool", "c1", kT[64:76, :, :, :].rearrange("p h s n -> p h (s n)"), khot_d, writes=b_kslot)

        norm_dma(xt, b_xt, lambda t: ctxb[t * 128:(t + 1) * 128, :], 2, "xl")
        norm_stats(xt, b_xt, 2)
        norm_tp(xt, b_xt, hxT, b_hx, 2, 1, lambda kt: SH1(kt, 1))
        norm_dma(xt, b_xt, xrows(0), 4, "xl")
        for mt in range(4):
            bi = BK.get()
            proj_fm(hxT, b_hx, winA, b_w, 512 + mt * 128, 256, bi)
            for hh in range(2):
                evac(bi, "act", lambda e, mt=mt, bi=bi, hh=hh: e.activation(kcT[hh * 64:(hh + 1) * 64, 2 * mt + hh, :],
                                                                            BK.t[bi][hh * 64:(hh + 1) * 64, 0:256], AF.Copy), writes=[b_ctx])
            BK.put(bi)
        for tt in range(2):
            bi = BK.get()
            P.mm([lambda e, kt=kt, tt=tt, bi=bi: e.matmul(BK.t[bi][:], hxT[:, kt, tt * 128:(tt + 1) * 128], winA[:, kt, 1024:1536],
                                                        start=(kt == 0), stop=(kt == 7)) for kt in range(8)],
                 reads=[b_hx, b_w], writes=[BK.buf[bi]])
            evac(bi, "act", lambda e, tt=tt, bi=bi: e.activation(
                vcaug[:, tt, :, 0:3:2, :], BK.t[bi][:].rearrange("p (a b d) -> p a b d", a=4, b=2), AF.Copy), writes=[b_ctx])
            BK.put(bi)
        norm_stats(xt, b_xt, 4)
        stop_a1 = KSTOP in ("A1a", "A1b", "A1c")

        def proj_units(j):
            s = j % 3
            qsl = j % 2
            norm_tp(xt, b_xt, hxT, b_hx, 4, 0, lambda kt: SH1(kt, 0), range(0, 4))
            yield
            norm_tp(xt, b_xt, hxT, b_hx, 4, 0, lambda kt: SH1(kt, 0), range(4, 8))
            if j + 1 < NCH:
                norm_dma(xt, b_xt, xrows(j + 1), 4, "xl")
            P.dma("sp", "rope", ropet[:], rope_d[j], writes=[b_rope])
            P.dma("pool", "bm%d" % qsl, qrT[64:76, :, qsl, :], bmask_d[j], writes=[b_qslot[qsl]])
            tiles = [(isk, mt) for isk in range(2) for mt in range(4)]
            st = {}

            def stage_p(n):
                isk, mt = tiles[n]
                bi = BK.get()
                proj_fm(hxT, b_hx, winA, b_w, isk * 512 + mt * 128, 512, bi)
                if isk == 0:
                    src, b_src = qun[:, mt, qsl, :], b_qun[qsl][mt]
                else:
                    src, b_src = kun[n % 2][:], b_kun[n % 2]
                P.op("act", lambda e, src=src, bi=bi: e.activation(src, BK.t[bi][:], AF.Copy), reads=[BK.buf[bi]], writes=[b_src])
                st[n] = (bi, src, b_src)

            def stage_x(n):
                isk, mt = tiles[n]
                bi, src, b_src = st.pop(n)
                bp = BK.get()
                P.mm([lambda e, src=src, bp=bp: e.matmul(BK.t[bp][:], permb[:], src, start=True, stop=True)],
                     reads=[b_src, b_const], writes=[BK.buf[bp]])
                u = n % 2
                P.op("dve", lambda e, u=u, bi=bi: e.tensor_tensor(t1[u][:], BK.t[bi][:], ropet[:, 0, :], ALU.mult),
                     reads=[BK.buf[bi], b_rope, b_src], writes=[b_t1[u]])
                BK.put(bi)
                P.op("dve", lambda e, u=u, bp=bp: e.tensor_tensor(t2[u][:], BK.t[bp][:], ropet[:, 1, :], ALU.mult),
                     reads=[BK.buf[bp], b_rope], writes=[b_t2[u]])
                BK.put(bp)
                if isk == 0:
                    dst, b_dst, dsl = qrT, b_qslot[qsl], qsl
                else:
                    dst, b_dst, dsl = kT, b_kslot[s], s
                for hh in range(2):
                    P.op("pool" if hh == 0 else "dve", lambda e, u=u, hh=hh, dst=dst, mt=mt, dsl=dsl: e.tensor_tensor(
                        dst[0:64, 2 * mt + hh, dsl, :], t1[u][hh * 64:(hh + 1) * 64, :], t2[u][hh * 64:(hh + 1) * 64, :], ALU.add),
                        reads=[b_t1[u], b_t2[u]], writes=[b_dst])

            stage_p(0)
            yield
            for n in range(8):
                if n + 1 < 8:
                    stage_p(n + 1)
                stage_x(n)
                yield
            for tt in range(4 if KSUB >= 3 else 0):
                bi = BK.get()
                P.mm([lambda e, kt=kt, tt=tt, bi=bi: e.matmul(BK.t[bi][:], hxT[:, kt, tt * 128:(tt + 1) * 128], winA[:, kt, 1024:1536],
                                                            start=(kt == 0), stop=(kt == 7)) for kt in range(8)],
                     reads=[b_hx, b_w], writes=[BK.buf[bi]])
                evac(bi, "act", lambda e, tt=tt, bi=bi: e.activation(
                    vaug[:, s * 4 + tt, :, 0:3:2, :], BK.t[bi][:].rearrange("p (a b d) -> p a b d", a=4, b=2), AF.Copy),
                    writes=[b_vslot[s]])
                BK.put(bi)
                yield

        def proj_chunk(j):
            for _ in proj_units(j):
                pass

        def attn_halfgroup(hg, side=None):
            qj = (hg + 1) // 2
            qs = qj % 2
            qo = 256 * ((hg + 1) % 2)
            items = [(hp, pr) for hp in range(4) for pr in range(4)]
            acc_bank = {}

            def issue_scores(idx):
                hp, pr = items[idx]
                for q in range(2):
                    h = 2 * hp + q
                    bi = BK.get()
                    mms = []
                    rd = [b_qslot[qs]]
                    for half in range(2):
                        if pr < 3:
                            t = 2 * pr + half
                            gi = 256 * hg + 128 * t
                            kj, ko = gi // 512, gi % 512
                            ks = kj % 3
                            rd.append(b_kslot[ks])
                            mms.append(lambda e, half=half, ks=ks, ko=ko, h=h, bi=bi: e.matmul(
                                BK.t[bi][:, half * 256:(half + 1) * 256], kT[0:76, h, ks, ko:ko + 128], qrT[0:76, h, qs, qo:qo + 256],
                                start=True, stop=True))
                        else:
                            rd.append(b_ctx)
                            rd.append(b_qun[qs][h // 2])
                            mms.append(lambda e, half=half, h=h, bi=bi: e.matmul(
                                BK.t[bi][:, half * 256:(half + 1) * 256], kcT[:, h, half * 128:(half + 1) * 128],
                                qun[:, h // 2, qs, qo:qo + 256], start=True, stop=True))
                    P.mm(mms, reads=rd, writes=[BK.buf[bi]])
                    eb = (idx % 3) * 2 + q
                    evac(bi, "act", lambda e, bi=bi, eb=eb: e.activation(Eb[eb][:], BK.t[bi][:], AF.Exp, scale=0.125), writes=[b_E[eb]])
                    BK.put(bi)
                    if pr < 3:
                        P.op("dve", lambda e, eb=eb, h=h, pr=pr: e.tensor_tensor(Pb[eb][:], Eb[eb][:], mtab[:, h, pr * 512:(pr + 1) * 512], ALU.mult),
                             reads=[b_E[eb], b_mtab], writes=[b_P[eb]])

            def issue_pv(idx):
                hp, pr = items[idx]
                if pr == 0:
                    acc_bank[hp] = [BK.get(), BK.get()]
                mms = []
                rd = []
                wr = []
                for half in range(2):
                    for q in range(2):
                        h = 2 * hp + q
                        ab = acc_bank[hp][q]
                        eb = (idx % 3) * 2 + q
                        col = q * 64
                        if pr < 3:
                            t = 2 * pr + half
                            gi = 256 * hg + 128 * t
                            kj, ko = gi // 512, gi % 512
                            vt = (kj % 3) * 4 + ko // 128
                            rd.append(b_vslot[kj % 3])
                            lhs = vaug[:, vt, hp, :, :].rearrange("p c d -> p (c d)")[:, col:col + 128]
                            rhs = Pb[eb][:, half * 256:(half + 1) * 256]
                            rd.append(b_P[eb])
                        else:
                            rd.append(b_ctx)
                            lhs = vcaug[:, half, hp, :, :].rearrange("p c d -> p (c d)")[:, col:col + 128]
                            rhs = Eb[eb][:, half * 256:(half + 1) * 256]
                            rd.append(b_E[eb])
                        wr.append(BK.buf[ab])
                        mms.append(lambda e, lhs=lhs, rhs=rhs, ab=ab, first=(pr == 0 and half == 0), last=(pr == 3 and half == 1): e.matmul(
                            BK.t[ab][:, 0:256], lhs, rhs, start=first, stop=last))
                P.mm(mms, reads=rd, writes=wr)
                if pr == 3:
                    for q in range(2):
                        h = 2 * hp + q
                        ab = acc_bank[hp][q]
                        ob = q * 64
                        sbase = 64 - ob
                        r = q
                        if q == 0:
                            P.op("dve", lambda e, ab=ab, ob=ob, sbase=sbase, r=r: e.reciprocal(rcp[r][ob:ob + 64, :], BK.t[ab][sbase:sbase + 64, 0:256]),
                                 reads=[BK.buf[ab]], writes=[b_rcp[r]])
                        else:
                            P.op("act", lambda e, ab=ab, ob=ob, sbase=sbase, r=r: e.activation(rcp[r][ob:ob + 64, :], BK.t[ab][sbase:sbase + 64, 0:256], AF.Ln),
                                 reads=[BK.buf[ab]], writes=[b_rcp[r]])
                            P.op("act", lambda e, ob=ob, r=r: e.activation(rcp[r][ob:ob + 64, :], rcp[r][ob:ob + 64, :], AF.Exp, scale=-1.0),
                                 reads=[b_rcp[r]], writes=[b_rcp[r]])
                        P.op("dve", lambda e, ab=ab, ob=ob, r=r, h=h: e.tensor_tensor(
                            attnT[ob:ob + 64, h // 2, 256 * hg:256 * hg + 256], BK.t[ab][ob:ob + 64, 0:256], rcp[r][ob:ob + 64, :], ALU.mult),
                            reads=[BK.buf[ab], b_rcp[r]], writes=[b_attn[hg]])
                        BK.put(ab)

            LA = 2
            n = len(items)
            for i in range(min(LA, n)):
                issue_scores(i)
            for i in range(n):
                if i + LA < n:
                    issue_scores(i + LA)
                issue_pv(i)
                if side is not None:
                    next(side, None)
            if side is not None:
                for _ in side:
                    pass

        proj_chunk(0)
        norm_stats(xt, b_xt, 4)
        proj_chunk(1)
        for g in range(8):
            attn_halfgroup(2 * g, norm_stats_units(xt, b_xt, 4) if g + 2 < NCH else None)
            attn_halfgroup(2 * g + 1, proj_units(g + 2) if g + 2 < NCH else None)
        P.barrier()

    if DEBUG:
        with contextlib.ExitStack() as DB:
            tmp = sb(DB, "dbgtmp", [128, 4, TOK], F32)
            bt = Buf("dbgtmp")
            P.op("dve", lambda e: e.tensor_copy(tmp[:], attnT[:]), reads=b_attn, writes=[bt])
            P.dma("sp", "dbg", dbg["attnT"], tmp[:], reads=[bt])
            P.barrier()

    print("op counts", P.cnt, {k: v[1] for k, v in P.dma_sems.items()})
    W.close()
    if KSTOP in ("A1", "A1a", "A1b", "A1c"):
        A.close()
        ES.close()
        return nc
    b_xnew = Buf("xnew")
    with contextlib.ExitStack() as A2:
        winB = sb(A2, "winB", [128, 8, 1024], BF16)
        b_wB = Buf("winB")
        wout = sb(A2, "wout", [128, 8, D], BF16)
        b_wout = Buf("w_out")
        for kt in range(0, 8, 4):
            P.dma("pool", "win", winB[:, kt:kt + 4, :], win_v[:, kt:kt + 4, 1536:2560], writes=[b_wB])
        P.dma("pool", "wout", wout[:], w_out.rearrange("(kt p) n -> p kt n", p=128), writes=[b_wout])
        g1B = sb(A2, "g1B", [128, D], F32)
        b_g1 = Buf("g1B")
        P.dma("sp", "c0", g1B[:], gB_d[0], writes=[b_g1])
        dwk_s = sb(A2, "dwk_s", [128, 4, 31], F32)
        cvp_s = sb(A2, "cvp_s", [128, 3, 4], F32)
        b_cv = Buf("cv")
        P.dma("sp", "c0", dwk_s[:], dwk, writes=[b_cv])
        P.dma("sp", "c0", cvp_s[:], cvp, writes=[b_cv])
        HR = 3 * 512 + 272
        hring = sb(A2, "hring", [128, 4, HR], BF16)
        b_hslot = [Buf("h0"), Buf("h1"), Buf("h2"), Buf("hext")]
        diag = sb(A2, "diag", [128, 4, 31, 128], BF16)
        identb = sb(A2, "identb", [128, 128], BF16)
        vms = sb(A2, "vms", [128, 2, 512], BF16)
        sg = [sb(A2, "sg%d" % i, [128, 512], F32) for i in range(2)]
        b_sg = [Buf("sg0"), Buf("sg1")]
        ybuf = sb(A2, "ybuf", [128, 4, 512], F32)
        ysq = sb(A2, "ysq", [128, 4, 512], F32)
        b_y = Buf("y")
        b_ysq = Buf("ysq")
        mstat = sb(A2, "mstat", [128, 4, 512], F32)
        b_ms = Buf("mstat")
        convT = sb(A2, "convT", [128, 2, 4, 512], BF16)
        b_conv = [Buf("convT0"), Buf("convT1")]
        tt_ = [sb(A2, "pt%d" % i, [128, D], F32) for i in range(2)]
        b_tt = [Buf("pt0"), Buf("pt1")]
        xres = [sb(A2, "xres%d" % i, [128, D], F32) for i in range(2)]
        b_xres = [Buf("xres0"), Buf("xres1")]
        b_diag = Buf("diag")
        b_vm = Buf("vm")
        P.dma("pool", "c1", identb[:], ident_d, writes=[b_diag])
        P.dma("pool", "c1", vms[:], vm_d.rearrange("a p n -> p a n"), writes=[b_vm])
        norm_dma(xt, b_xt, xrows(0), 4, "xl")
        for ct in range(4):
            for tap in range(31):
                P.op("dve", lambda e, ct=ct, tap=tap: e.tensor_scalar(diag[:, ct, tap, :], identb[:], dwk_s[:, ct, tap:tap + 1], None, ALU.mult),
                     reads=[b_diag, b_cv], writes=[b_diag])
        norm_stats(xt, b_xt, 4)

        def glu_chunk(j):
            s = j % 3
            norm_tp(xt, b_xt, hxT, b_hx, 4, 0, lambda kt: SH1(kt, 0))
            if j + 1 < NCH:
                norm_dma(xt, b_xt, xrows(j + 1), 4, "xl")
            for ct in range(4):
                ba = BK.get()
                proj_fm(hxT, b_hx, winB, b_wB, ct * 128, 512, ba)
                bb = BK.get()
                proj_fm(hxT, b_hx, winB, b_wB, 512 + ct * 128, 512, bb)
                u = ct % 2
                evac(bb, "act", lambda e, bb=bb, u=u: e.activation(sg[u][:], BK.t[bb][:], AF.Sigmoid), writes=[b_sg[u]])
                BK.put(bb)
                evac(ba, "dve", lambda e, ba=ba, u=u, ct=ct: e.tensor_tensor(hring[:, ct, s * 512:(s + 1) * 512], BK.t[ba][:], sg[u][:], ALU.mult),
                     reads=[b_sg[u]], writes=[b_hslot[s]])
                BK.put(ba)
            if j in (0, NCH - 1):
                w = 0 if j == 0 else 1
                for ct in range(4):
                    P.op("pool", lambda e, w=w, ct=ct: e.tensor_tensor(hring[:, ct, s * 512:(s + 1) * 512], hring[:, ct, s * 512:(s + 1) * 512],
                                                                    vms[:, w, :], ALU.mult),
                         reads=[b_hslot[s], b_vm], writes=[b_hslot[s]])
            if s == 0:
                P.op("pool", lambda e: e.tensor_copy(hring[:, :, 1536:1536 + 272], hring[:, :, 0:272]), reads=[b_hslot[0]], writes=[b_hslot[3]])

        def conv_group(g):
            s = g % 3
            base = s * 512 + 241
            rd = [b_hslot[s], b_hslot[s + 1] if s < 2 else b_hslot[3], b_diag]
            for ct in range(4):
                bi = BK.get()
                P.mm([lambda e, tap=tap, ct=ct, bi=bi: e.matmul(BK.t[bi][:], diag[:, ct, tap, :], hring[:, ct, base + tap:base + tap + 512],
                                                               start=(tap == 0), stop=(tap == 30)) for tap in range(31)],
                     reads=rd, writes=[BK.buf[bi]])
                P.op("act", lambda e, ct=ct, bi=bi: e.activation(ybuf[:, ct, :], BK.t[bi][:], AF.Identity, bias=cvp_s[:, 0, ct:ct + 1]),
                     reads=[BK.buf[bi], b_cv], writes=[b_y])
                evac(bi, "act", lambda e, ct=ct, bi=bi: e.activation(ysq[:, ct, :], BK.t[bi][:], AF.Square, bias=cvp_s[:, 0, ct:ct + 1]),
                     reads=[b_cv], writes=[b_ysq])
                BK.put(bi)
            bs = BK.get()
            P.mm([lambda e, ct=ct: e.matmul(BK.t[bs][:], onesf[:], ybuf[:, ct, :], start=(ct == 0), stop=(ct == 3)) for ct in range(4)],
                 reads=[b_y, b_const], writes=[BK.buf[bs]])
            bq = BK.get()
            P.mm([lambda e, ct=ct: e.matmul(BK.t[bq][:], onesf[:], ysq[:, ct, :], start=(ct == 0), stop=(ct == 3)) for ct in range(4)],
                 reads=[b_ysq, b_const], writes=[BK.buf[bq]])
            P.op("act", lambda e: e.activation(mstat[:, 0, :], BK.t[bs][:], AF.Copy, scale=1.0 / 512), reads=[BK.buf[bs]], writes=[b_ms])
            evac(bs, "act", lambda e: e.activation(mstat[:, 1, :], BK.t[bs][:], AF.Square, scale=1.0 / 512), writes=[b_ms])
            BK.put(bs)
            evac(bq, "dve", lambda e: e.scalar_tensor_tensor(mstat[:, 2, :], BK.t[bq][:], 1.0 / 512, mstat[:, 1, :], ALU.mult, ALU.subtract),
                 reads=[b_ms], writes=[b_ms])
            BK.put(bq)
            P.op("act", lambda e: e.activation(mstat[:, 2, :], mstat[:, 2, :], AF.Ln, bias=EPS), reads=[b_ms], writes=[b_ms])
            P.op("act", lambda e: e.activation(mstat[:, 3, :], mstat[:, 2, :], AF.Exp, scale=-0.5), reads=[b_ms], writes=[b_ms])

        def conv_group_b(g):
            for ct in range(4):
                eng = "dve" if ct != 3 else "pool"
                P.op(eng, lambda e, ct=ct: e.tensor_tensor(ybuf[:, ct, :], ybuf[:, ct, :], mstat[:, 0, :], ALU.subtract),
                     reads=[b_y, b_ms], writes=[b_y])
                P.op(eng, lambda e, ct=ct: e.tensor_tensor(ybuf[:, ct, :], ybuf[:, ct, :], mstat[:, 3, :], ALU.mult),
                     reads=[b_y, b_ms], writes=[b_y])
                P.op("act", lambda e, ct=ct: e.activation(convT[:, g % 2, ct, :], ybuf[:, ct, :], AF.Silu, scale=cvp_s[:, 1, ct:ct + 1],
                                                          bias=cvp_s[:, 2, ct:ct + 1]),
                     reads=[b_y, b_cv], writes=[b_conv[g % 2]])
            if DEBUG and g == 0:
                P.dma("sp", "dbg", dbg["mstat"], mstat[:], reads=[b_ms])
                P.dma("sp", "dbg", dbg["ysq"], ysq[:], reads=[b_ysq])
                P.dma("sp", "dbg", dbg["hring"], hring[:], reads=b_hslot)
                P.barrier()
                P.op("dve", lambda e: e.tensor_copy(ysq[:], convT[:, 0]), reads=[b_conv[0]], writes=[b_ysq])
                P.dma("sp", "dbg", dbg["convT"], ysq[:], reads=[b_ysq])

        def outproj_group(g):
            for tt in range(4):
                u = tt % 2
                tok0 = 512 * g + tt * 128
                P.dma("sp", "xres%d" % u, xres[u][:], xe[HALO + tok0:HALO + tok0 + 128, :], writes=[b_xres[u]])
                for nh in range(2):
                    bi = BK.get()
                    mms = []
                    for kt in range(8):
                        if kt < 4:
                            lhs = attnT[:, kt, tok0:tok0 + 128]
                        else:
                            lhs = convT[:, g % 2, kt - 4, tt * 128:(tt + 1) * 128]
                        mms.append(lambda e, lhs=lhs, kt=kt, nh=nh, bi=bi: e.matmul(BK.t[bi][:], lhs, wout[:, kt, nh * 512:(nh + 1) * 512],
                                                                                 start=(kt == 0), stop=(kt == 7)))
                    P.mm(mms, reads=[b_conv[g % 2], b_wout] + b_attn, writes=[BK.buf[bi]])
                    evac(bi, "dve", lambda e, bi=bi, nh=nh, u=u: e.tensor_tensor(tt_[u][:, nh * 512:(nh + 1) * 512], BK.t[bi][:],
                                                                              g1B[:, nh * 512:(nh + 1) * 512], ALU.mult),
                         reads=[b_g1], writes=[b_tt[u]])
                    BK.put(bi)
                P.op("pool", lambda e, u=u: e.tensor_tensor(xres[u][:], tt_[u][:], xres[u][:], ALU.add),
                     reads=[b_tt[u], b_xres[u]], writes=[b_xres[u]])
                P.dma("sp", "xst%d" % u, xnew_d[tok0:tok0 + 128, :], xres[u][:], reads=[b_xres[u]], writes=[b_xnew])
                if DEBUG:
                    P.dma("sp", "dbg", dbg["xnew"][tok0:tok0 + 128, :], xres[u][:], reads=[b_xres[u]])

        glu_chunk(0)
        norm_stats(xt, b_xt, 4)
        glu_chunk(1)
        norm_stats(xt, b_xt, 4)
        for g in range(8):
            conv_group(g)
            if g + 2 < NCH:
                glu_chunk(g + 2)
            if g >= 1:
                outproj_group(g - 1)
            conv_group_b(g)
            if g + 3 < NCH:
                norm_stats(xt, b_xt, 4)
        outproj_group(7)
        P.barrier()
    A.close()
    if KSTOP == "A2":
        ES.close()
        return nc

    with contextlib.ExitStack() as B:
        CB = 256
        NCB = TOK // CB
        w1s = sb(B, "w1s", [128, 8, 4 * D], BF16)
        w2s = sb(B, "w2s", [128, 32, D], BF16)
        b_w1 = Buf("w1")
        b_w2 = Buf("w2")
        w1_v = w1.rearrange("(kt p) n -> p kt n", p=128)
        w2_v = w2.rearrange("(kt p) n -> p kt n", p=128)
        for kt in range(0, 8, 2):
            P.dma("pool", "w1", w1s[:, kt:kt + 2, :], w1_v[:, kt:kt + 2, :], writes=[b_w1])
        for kt in range(0, 32, 8):
            P.dma("pool", "w2", w2s[:, kt:kt + 8, :], w2_v[:, kt:kt + 8, :], writes=[b_w2])
        xtB = sb(B, "xtB", [128, 2, D], F32)
        b_xtB = Buf("xtB")
        hx2T = sb(B, "hx2T", [128, 8, CB], BF16)
        b_hx2 = Buf("hx2")
        hT = sb(B, "hT", [128, 32, CB], BF16)
        b_hT = Buf("hT")
        fgB = sb(B, "fgBs", [128, D], F32)
        g2B = sb(B, "g2B", [128, D], F32)
        b_fg = Buf("fg")
        sq = [sb(B, "sq%d" % i, [128, CB], F32) for i in range(2)]
        b_sq = [Buf("sq0"), Buf("sq1")]
        xr = [sb(B, "xr%d" % i, [128, D], F32) for i in range(2)]
        b_xr = [Buf("xr0"), Buf("xr1")]
        pt = [sb(B, "ptB%d" % i, [128, D], F32) for i in range(2)]
        b_pt = [Buf("ptB0"), Buf("ptB1")]
        ot = [sb(B, "ot%d" % i, [128, D], F32) for i in range(2)]
        b_ot = [Buf("ot0"), Buf("ot1")]
        fst = sb(B, "fst", [128, 8], F32)
        b_fst = Buf("fst")
        P.dma("sp", "c0", fgB[:], fgB_d, writes=[b_fg])
        P.dma("sp", "c0", g2B[:], gB_d[1], writes=[b_fg])
        b_outd = Buf("outd")
        nrows = lambda c: (lambda t: xnew_d[c * CB + t * 128:c * CB + (t + 1) * 128, :])

        def mlp1(c):
            for fp in range(16):
                bks = [BK.get(), BK.get()]
                mms = []
                for kt in range(8):
                    for q in range(2):
                        ft = 2 * fp + q
                        mms.append(lambda e, kt=kt, ft=ft, bi=bks[q]: e.matmul(BK.t[bi][:, 0:CB], w1s[:, kt, ft * 128:(ft + 1) * 128], hx2T[:, kt, :],
                                                                               start=(kt == 0), stop=(kt == 7)))
                P.mm(mms, reads=[b_hx2, b_w1], writes=[BK.buf[bks[0]], BK.buf[bks[1]]])
                for q in range(2):
                    ft = 2 * fp + q
                    bi = bks[q]
                    u = ft % 2
                    P.op("act", lambda e, bi=bi, u=u: e.activation(sq[u][:], BK.t[bi][:, 0:CB], AF.Square), reads=[BK.buf[bi]], writes=[b_sq[u]])
                    evac(bi, "dve", lambda e, bi=bi, u=u, ft=ft: e.scalar_tensor_tensor(hT[:, ft, :], BK.t[bi][:, 0:CB], 0.0, sq[u][:], ALU.is_gt, ALU.mult),
                         reads=[b_sq[u]], writes=[b_hT])
                    BK.put(bi)

        def mlp2(c):
            for tt in range(CB // 128):
                u = tt % 2
                tok0 = c * CB + tt * 128
                P.dma("sp", "xr%d" % u, xr[u][:], xnew_d[tok0:tok0 + 128, :], writes=[b_xr[u]])
                for nh in range(2):
                    bi = BK.get()
                    P.mm([lambda e, ft=ft, tt=tt, nh=nh, bi=bi: e.matmul(BK.t[bi][:], hT[:, ft, tt * 128:(tt + 1) * 128],
                                                                     w2s[:, ft, nh * 512:(nh + 1) * 512], start=(ft == 0), stop=(ft == 31))
                          for ft in range(32)], reads=[b_hT, b_w2], writes=[BK.buf[bi]])
                    evac(bi, "dve", lambda e, bi=bi, nh=nh, u=u: e.tensor_tensor(pt[u][:, nh * 512:(nh + 1) * 512], BK.t[bi][:],
                                                                              g2B[:, nh * 512:(nh + 1) * 512], ALU.mult),
                         reads=[b_fg], writes=[b_pt[u]])
                    BK.put(bi)
                P.op("pool", lambda e, u=u: e.tensor_tensor(pt[u][:], pt[u][:], xr[u][:], ALU.add),
                     reads=[b_pt[u], b_xr[u]], writes=[b_pt[u]])
                P.op("act", lambda e, u=u: e.activation(junk[:], pt[u][:], AF.Square, accum_out=fst[:, u:u + 1]),
                     reads=[b_pt[u], b_junk], writes=[b_fst, b_junk])
                P.op("act", lambda e, u=u: e.activation(fst[:, 2 + u:3 + u], fst[:, u:u + 1], AF.Ln, scale=1.0 / D, bias=EPS),
                     reads=[b_fst], writes=[b_fst])
                P.op("act", lambda e, u=u: e.activation(fst[:, 4 + u:5 + u], fst[:, 2 + u:3 + u], AF.Exp, scale=-0.5), reads=[b_fst], writes=[b_fst])
                P.op("dve", lambda e, u=u: e.scalar_tensor_tensor(ot[u][:], pt[u][:], fst[:, 4 + u:5 + u], fgB[:], ALU.mult, ALU.mult),
                     reads=[b_pt[u], b_fst, b_fg], writes=[b_ot[u]])
                P.dma("sp", "ost%d" % u, out_d[tok0:tok0 + 128, :], ot[u][:], reads=[b_ot[u]], writes=[b_outd])

        norm_dma(xtB, b_xtB, nrows(0), 2, "xlB")
        norm_stats(xtB, b_xtB, 2)
        norm_tp(xtB, b_xtB, hx2T, b_hx2, 2, 2, lambda kt: SH2(kt))
        norm_dma(xtB, b_xtB, nrows(1), 2, "xlB")
        for c in range(NCB):
            if c + 1 < NCB:
                norm_stats(xtB, b_xtB, 2)
            mlp1(c)
            if c + 1 < NCB:
                norm_tp(xtB, b_xtB, hx2T, b_hx2, 2, 2, lambda kt: SH2(kt))
                if c + 2 < NCB:
                    norm_dma(xtB, b_xtB, nrows(c + 2), 2, "xlB")
            mlp2(c)
        P.barrier()
    ES.close()
    print("op counts", P.cnt, {k: v[1] for k, v in P.dma_sems.items()})
    return nc


def _consts(half):
    row0 = half * 64
    ti = np.arange(TE)
    gt = half * TOK + ti - HALO
    grow = np.floor_divide(gt, 64).astype(np.float32)
    gcol = np.mod(gt, 64).astype(np.float32)
    inv = (np.float32(10000.0) ** (-np.arange(16, dtype=np.float32) / np.float32(16))).astype(np.float32)
    rope = np.zeros((128, 2, TE), np.float32)
    for p in range(128):
        d = p % 64
        pos = grow if d < 32 else gcol
        ang = (pos * inv[d % 16]).astype(np.float32)
        rope[p, 0] = np.cos(ang)
        sn = np.sin(ang)
        rope[p, 1] = -sn if (d % 32) < 16 else sn
    rope = np.ascontiguousarray(rope.reshape(128, 2, NCH, 512).transpose(2, 0, 1, 3))
    perm = np.zeros((128, 128), np.float32)
    for m in range(128):
        k = m + 16 if (m % 32) < 16 else m - 16
        perm[k, m] = 1.0
    khot = np.zeros((12, 8, 3, 512), np.float32)
    for s in range(3):
        for i in range(512):
            khot[(8 * s - 4 + i // 64) % 12, :, s, i] = 1.0
    khot = khot.reshape(12, 8, 1536)
    bmask = np.zeros((NCH, 12, 8, 512), np.float32)
    for j in range(NCH):
        for rr in range(8):
            tau = -HALO + 512 * j + 64 * rr
            if tau < 0 or tau >= TOK:
                continue
            lr = tau // 64
            r0 = 4 * (lr // 4)
            gr = row0 + lr
            rs = min(max(gr - 4, 0), 120)
            for kr in range(r0 - 4, r0 + 8):
                gk = row0 + kr
                ok = rs <= gk < rs + 8
                bmask[j, kr % 12, :, rr * 64:(rr + 1) * 64] = 0.0 if ok else NEG
    vm = np.ones((2, 128, 512), np.float32)
    g0 = half * TOK - HALO + np.arange(512)
    vm[0, :, :] = (g0 >= 0).astype(np.float32)[None, :]
    g8 = half * TOK - HALO + 512 * 8 + np.arange(512)
    vm[1, :, :] = (g8 < SEQ).astype(np.float32)[None, :]
    return rope, perm, khot, bmask, vm


def _mtab(rpb):
    tab = np.full((128, 8, 6, 4, 64), NEG, np.float32)
    qc = np.arange(64)
    cs = np.clip(qc - 8, 0, 48)
    for t in range(6):
        for i in range(2):
            for rq in range(4):
                dl = -4 + 2 * t + i - rq
                if dl < -7 or dl > 7:
                    continue
                for kc in range(64):
                    ok = (kc >= cs) & (kc < cs + 16)
                    co = np.clip(kc - qc + 15, 0, 30)
                    vals = rpb[:, dl + 7, :][:, co]
                    tab[i * 64 + kc, :, t, rq, :] = np.where(ok[None, :], vals, NEG)
    return tab.reshape(128, 8, 1536)


_CACHE = {}


def kernel(x, c, ctx, c_ctx, w_ada, b_ada, norm1_g, norm2_g, w_in, dw_kernel, dw_bias,
           conv_ln_g, conv_ln_b, rpb, w_out, w_mlp1, w_mlp2, final_g):
    f = lambda a: np.ascontiguousarray(np.asarray(a, dtype=np.float32))
    x, c, ctx, c_ctx = f(x), f(c), f(ctx), f(c_ctx)
    w_ada0, b_ada0 = f(w_ada)[0], f(b_ada)[0]
    fm = lambda v, n: np.ascontiguousarray(v.reshape(n, 128).T)
    shared = {
        "w_ada": w_ada0,
        "b_adaT": fm(b_ada0, 48),
        "b_ada_g": np.ascontiguousarray(np.concatenate([b_ada0[2 * D:3 * D], b_ada0[5 * D:6 * D]])[None, :]),
        "n1gT": fm(f(norm1_g)[0], 8),
        "n2gT": fm(f(norm2_g)[0], 8),
        "fgB": np.ascontiguousarray(np.broadcast_to(f(final_g)[None, :], (128, D))),
        "w_in": f(w_in)[0],
        "w_out": f(w_out)[0],
        "w1": f(w_mlp1)[0],
        "w2": f(w_mlp2)[0],
        "dwk": np.ascontiguousarray(f(dw_kernel)[0].reshape(31, 4, 128).transpose(2, 1, 0)),
        "cvp": np.ascontiguousarray(np.stack([fm(f(dw_bias)[0], 4), fm(f(conv_ln_g)[0], 4), fm(f(conv_ln_b)[0], 4)], axis=1)),
        "ident": np.eye(128, dtype=np.float32),
        "mtab": _mtab(f(rpb)[0]),
    }
    in_maps = []
    for core in range(8):
        b, half = core // 2, core % 2
        rope, perm, khot, bmask, vm = _consts(half)
        xe = np.zeros((TE, D), np.float32)
        lo = half * TOK - HALO
        a0, a1 = max(lo, 0), min(lo + TE, SEQ)
        xe[a0 - lo:a1 - lo] = x[b, a0:a1]
        cv = np.stack([fm(c[b], 8), fm(c_ctx, 8)], axis=2)
        m = dict(shared)
        m.update({"xe": xe, "ctxb": np.ascontiguousarray(ctx[b]), "cvec": np.ascontiguousarray(cv), "perm": perm,
                  "rope": rope, "khot": khot, "bmask": bmask, "vm": vm})
        in_maps.append(m)
    if "nc" not in _CACHE:
        _CACHE["nc"] = build_nc()
    ncore = int(os.environ.get("KCORES", "8"))
    res = run_bass_kernel_spmd(_CACHE["nc"], in_maps[:ncore], core_ids=list(range(ncore)))
    out = np.zeros((NB, SEQ, D), np.float32)
    for core in range(ncore):
        b, half = core // 2, core % 2
        out[b, half * TOK:(half + 1) * TOK] = res.results[core]["out"]
    if DEBUG:
        kernel.dbg = res.results
    return out
```

```python
import contextlib
import os
import numpy as np
import concourse.bass as bass
import concourse.mybir as mybir
from concourse.bass_utils import run_bass_kernel_spmd

F32 = mybir.dt.float32
BF16 = mybir.dt.bfloat16
AF = mybir.ActivationFunctionType
ALU = mybir.AluOpType

D = 1024
SEQ = 8192
NB = 4
TOK = 4096
HALO = 256
TE = TOK + 2 * HALO
NCH = 9
CTX = 256
EPS = 1e-6
NEG = -30000.0
DEBUG = bool(int(os.environ.get("KDEBUG", "0")))
KSTOP = os.environ.get("KSTOP", "")
KSUB = int(os.environ.get("KSUB", "9"))


class Buf:
    def __init__(self, name=""):
        self.name = name
        self.ws = {}
        self.rs = {}


class Prog:
    def __init__(self, nc):
        self.nc = nc
        self.engs = {"pe": nc.tensor, "act": nc.scalar, "dve": nc.vector, "pool": nc.gpsimd, "sp": nc.sync}
        self.sems = {k: nc.alloc_semaphore("sem_" + k) for k in self.engs}
        self.cnt = {k: 0 for k in self.engs}
        self.waited = {k: {} for k in self.engs}
        self.dma_sems = {}

    def _wait(self, e, deps):
        for key, (sem, val) in deps.items():
            if self.waited[e].get(key, 0) < val:
                self.engs[e].wait_ge(sem, val)
                self.waited[e][key] = val

    def _deps(self, reads, writes):
        deps = {}
        for b in reads:
            for k, v in b.ws.items():
                if k not in deps or deps[k][1] < v[1]:
                    deps[k] = v
        for b in writes:
            for k, v in b.rs.items():
                if k not in deps or deps[k][1] < v[1]:
                    deps[k] = v
        return deps

    def _mark(self, key, tok, reads, writes):
        for b in writes:
            if b.rs:
                b.ws = {}
                b.rs = {}
            b.ws[key] = tok
        for b in reads:
            b.rs[key] = tok

    def op(self, e, fn, reads=(), writes=()):
        self._wait(e, self._deps(reads, writes))
        inst = fn(self.engs[e])
        self.cnt[e] += 1
        inst.then_inc(self.sems[e], 1)
        self._mark(e, (self.sems[e], self.cnt[e]), reads, writes)

    def mm(self, mms, reads=(), writes=()):
        self._wait("pe", self._deps(reads, writes))
        inst = None
        for fn in mms:
            inst = fn(self.engs["pe"])
        self.cnt["pe"] += 1
        inst.then_inc(self.sems["pe"], 1)
        self._mark("pe", (self.sems["pe"], self.cnt["pe"]), reads, writes)

    def dma(self, e, key, out, in_, reads=(), writes=()):
        if key in ("c0", "c1"):
            self.nuniq = getattr(self, "nuniq", 0) + 1
            key = "u%d" % self.nuniq
        self._wait(e, self._deps(reads, writes))
        if key not in self.dma_sems:
            self.dma_sems[key] = [self.nc.alloc_semaphore("sd_" + key), 0]
        s = self.dma_sems[key]
        s[1] += 16
        self.engs[e].dma_start(out=out, in_=in_).then_inc(s[0], 16)
        self._mark("dma_" + key, (s[0], s[1]), reads, writes)

    def barrier(self):
        deps = {k: (self.sems[k], self.cnt[k]) for k in self.engs if self.cnt[k] > 0}
        for key, s in self.dma_sems.items():
            deps["dma_" + key] = (s[0], s[1])
        for e in self.engs:
            self._wait(e, deps)


class Banks:
    def __init__(self, nc):
        self.t = [nc.alloc_psum_tensor("bank%d" % i, [128, 512], F32) for i in range(8)]
        self.buf = [Buf("bank%d" % i) for i in range(8)]
        self.free = list(range(8))

    def get(self):
        assert self.free, "out of PSUM banks"
        i = self.free.pop(0)
        return i

    def put(self, i):
        self.free.append(i)


def build_nc():
    nc = bass.Bass("TRN2", target_bir_lowering=False)
    P = Prog(nc)
    BK = Banks(nc)

    def din(name, shape, dt=F32):
        return nc.dram_tensor(name, list(shape), dt, kind="ExternalInput").ap()

    xe = din("xe", [TE, D])
    ctxb = din("ctxb", [CTX, D])
    cvec = din("cvec", [128, 8, 2])
    w_ada = din("w_ada", [D, 6 * D])
    b_adaT = din("b_adaT", [128, 48])
    b_ada_g = din("b_ada_g", [1, 2 * D])
    n1gT = din("n1gT", [128, 8])
    n2gT = din("n2gT", [128, 8])
    fgB_d = din("fgB", [128, D])
    w_in = din("w_in", [D, 2560])
    w_out = din("w_out", [D, D])
    w1 = din("w1", [D, 4 * D])
    w2 = din("w2", [4 * D, D])
    dwk = din("dwk", [128, 4, 31])
    cvp = din("cvp", [128, 3, 4])
    ident_d = din("ident", [128, 128])
    perm_d = din("perm", [128, 128])
    rope_d = din("rope", [NCH, 128, 2, 512])
    mtab_d = din("mtab", [128, 8, 1536])
    khot_d = din("khot", [12, 8, 1536])
    bmask_d = din("bmask", [NCH, 12, 8, 512])
    vm_d = din("vm", [2, 128, 512])
    out_d = nc.dram_tensor("out", [TOK, D], F32, kind="ExternalOutput").ap()
    xnew_d = nc.dram_tensor("xnew", [TOK, D], F32).ap()
    dbg = {}
    if DEBUG:
        dbg["attnT"] = nc.dram_tensor("dbg_attnT", [128, 4, TOK], F32, kind="ExternalOutput").ap()
        dbg["modT"] = nc.dram_tensor("dbg_modT", [128, 64], F32, kind="ExternalOutput").ap()
        dbg["g1B"] = nc.dram_tensor("dbg_g1B", [128, D], F32, kind="ExternalOutput").ap()
        dbg["hx0"] = nc.dram_tensor("dbg_hx0", [128, 8, 512], F32, kind="ExternalOutput").ap()
        dbg["xnew"] = nc.dram_tensor("dbg_xnew", [TOK, D], F32, kind="ExternalOutput").ap()
        dbg["convT"] = nc.dram_tensor("dbg_convT", [128, 4, 512], F32, kind="ExternalOutput").ap()
        dbg["q0"] = nc.dram_tensor("dbg_q0", [128, 8, 512], F32, kind="ExternalOutput").ap()
        dbg["mstat"] = nc.dram_tensor("dbg_mstat", [128, 4, 512], F32, kind="ExternalOutput").ap()
        dbg["ysq"] = nc.dram_tensor("dbg_ysq", [128, 4, 512], F32, kind="ExternalOutput").ap()
        dbg["hring"] = nc.dram_tensor("dbg_hring", [128, 4, 1808], BF16, kind="ExternalOutput").ap()

    ES = contextlib.ExitStack()

    def sb(stack, name, shape, dt):
        return stack.enter_context(nc.sbuf_tensor("s_" + name, list(shape), dt))

    identf = sb(ES, "identf", [128, 128], F32)
    onesf = sb(ES, "onesf", [128, 128], F32)
    onesb = sb(ES, "onesb", [128, 128], BF16)
    permb = sb(ES, "permb", [128, 128], BF16)
    modT = sb(ES, "modT", [128, 32, 2], F32)
    gsT = sb(ES, "gsT", [128, 3, 8], F32)
    n1g = sb(ES, "n1g", [128, 8], F32)
    n2g = sb(ES, "n2g", [128, 8], F32)
    stat = sb(ES, "stat", [128, 16], F32)
    junk = sb(ES, "junk", [128, D], BF16)
    gB_d = nc.dram_tensor("gB_scr", [2, 128, D], F32).ap()
    b_const = Buf("const")
    b_mod = Buf("mod")
    b_stat = Buf("stat")
    b_junk = Buf("junk")

    P.dma("sp", "c0", identf[:], ident_d, writes=[b_const])
    P.dma("pool", "c1", permb[:], perm_d, writes=[b_const])
    P.dma("sp", "c0", n1g[:], n1gT, writes=[b_const])
    P.dma("sp", "c0", n2g[:], n2gT, writes=[b_const])
    P.op("dve", lambda e: e.memset(onesf[:], 1.0), writes=[b_const])
    P.op("dve", lambda e: e.memset(onesb[:], 1.0), writes=[b_const])

    def evac(i, fn_engine, fn, reads=(), writes=()):
        P.op(fn_engine, fn, reads=[BK.buf[i]] + list(reads), writes=writes)

    A = contextlib.ExitStack()
    attnT = sb(A, "attnT", [128, 4, TOK], BF16)
    xt = sb(A, "xt", [128, 4, D], F32)
    b_xt = Buf("xt")
    hxT = sb(A, "hxT", [128, 8, 512], BF16)
    b_hx = Buf("hx")
    b_attn = [Buf("attn%d" % i) for i in range(16)]
    win_v = w_in.rearrange("(kt p) n -> p kt n", p=128)
    xrows = lambda j: (lambda t: xe[j * 512 + t * 128:j * 512 + (t + 1) * 128, :])

    W = contextlib.ExitStack()
    winA = sb(W, "winA", [128, 8, 1536], BF16)
    b_w = Buf("winA")
    mtab = sb(W, "mtab", [128, 8, 1536], BF16)
    b_mtab = Buf("mtab")
    mt_src = mtab_d.rearrange("p h n -> p (h n)")
    mt_dst = mtab[:].rearrange("p h n -> p (h n)")
    xt_flat = xt[:].rearrange("p t n -> p (t n)")
    for r3 in range(3):
        P.dma("sp", "xl", xt_flat, mt_src[:, r3 * 4096:(r3 + 1) * 4096], writes=[b_xt])
        P.op("act", lambda e, r3=r3: e.activation(mt_dst[:, r3 * 4096:(r3 + 1) * 4096], xt_flat, AF.Exp), reads=[b_xt], writes=[b_mtab])

    with contextlib.ExitStack() as S:
        cv = sb(S, "cv", [128, 8, 2], F32)
        scT = sb(S, "scT", [128, 8, 2], BF16)
        scTf = sb(S, "scTf", [128, 8, 2], F32)
        scB = sb(S, "scB", [128, 8, 128], BF16)
        badaT = sb(S, "badaT", [128, 48], F32)
        badag = sb(S, "badag", [1, 2 * D], BF16)
        gB = sb(S, "gB", [128, 2, D], F32)
        slab = [sb(S, "slab%d" % i, [128, 8, D], BF16) for i in range(2)]
        b_slab = [Buf("slab0"), Buf("slab1")]
        b_s = Buf("s_misc")
        b_gB = Buf("gB")
        P.dma("sp", "c0", cv[:], cvec, writes=[b_s])
        P.dma("sp", "c0", badaT[:], b_adaT, writes=[b_s])
        P.dma("pool", "c1", badag[:], b_ada_g, writes=[b_s])
        wada_v = w_ada.rearrange("(kt p) n -> p kt n", p=128)
        order = [0, 1, 3, 4, 2, 5]
        P.dma("pool", "slab0", slab[0][:], wada_v[:, :, order[0] * D:(order[0] + 1) * D], writes=[b_slab[0]])
        P.dma("pool", "slab1", slab[1][:], wada_v[:, :, order[1] * D:(order[1] + 1) * D], writes=[b_slab[1]])
        PREFETCH_WIN = True
        P.op("act", lambda e: e.activation(scTf[:], cv[:], AF.Silu), reads=[b_s], writes=[b_s])
        P.op("act", lambda e: e.activation(scT[:], cv[:], AF.Silu), reads=[b_s], writes=[b_s])
        for kt in range(8):
            P.op("dve", lambda e, kt=kt: e.tensor_scalar(scB[:, kt, :], onesb[:], scTf[:, kt, 0:1], None, ALU.mult),
                 reads=[b_s, b_const], writes=[b_s])
        bi = BK.get()
        for n, v in enumerate(order):
            s = n % 2
            if v in (0, 1, 3, 4):
                vi = {0: 0, 1: 1, 3: 2, 4: 3}[v]
                mms = []
                for mt in range(8):
                    for kt in range(8):
                        mms.append(lambda e, mt=mt, kt=kt, vi=vi, s=s: e.matmul(
                            BK.t[bi][:, (vi * 8 + mt) * 2:(vi * 8 + mt) * 2 + 2], slab[s][:, kt, mt * 128:(mt + 1) * 128],
                            scT[:, kt, :], start=(kt == 0), stop=(kt == 7)))
                P.mm(mms, reads=[b_slab[s], b_s], writes=[BK.buf[bi]])
            else:
                gi = 0 if v == 2 else 1
                for nh in range(2):
                    bj = BK.get()
                    mms = []
                    for kt in range(8):
                        mms.append(lambda e, kt=kt, s=s, nh=nh, bj=bj: e.matmul(
                            BK.t[bj][:], scB[:, kt, :], slab[s][:, kt, nh * 512:(nh + 1) * 512], start=(kt == 0), stop=False))
                    mms.append(lambda e, nh=nh, bj=bj, gi=gi: e.matmul(
                        BK.t[bj][:], onesb[0:1, :], badag[0:1, gi * D + nh * 512:gi * D + (nh + 1) * 512], start=False, stop=True))
                    P.mm(mms, reads=[b_slab[s], b_s, b_const], writes=[BK.buf[bj]])
                    evac(bj, "act", lambda e, bj=bj, nh=nh, gi=gi: e.activation(gB[:, gi, nh * 512:(nh + 1) * 512], BK.t[bj][:], AF.Copy),
                         writes=[b_gB])
                    BK.put(bj)
            if n + 2 < 6:
                v2 = order[n + 2]
                P.dma("pool", "slab%d" % s, slab[s][:], wada_v[:, :, v2 * D:(v2 + 1) * D], writes=[b_slab[s]])
        for kt in range(0, 8, 2):
            P.dma("pool", "win", winA[:, kt:kt + 2, :], win_v[:, kt:kt + 2, 0:1536], writes=[b_w])
        for vi, v in enumerate([0, 1, 3, 4]):
            for w in range(2):
                evac(bi, "dve", lambda e, vi=vi, v=v, w=w: e.tensor_tensor(
                    modT[:, vi * 8:(vi + 1) * 8, w],
                    BK.t[bi][:, vi * 16:(vi + 1) * 16].rearrange("p (m w) -> p m w", w=2)[:, :, w],
                    badaT[:, v * 8:(v + 1) * 8], ALU.add), reads=[b_s], writes=[b_mod])
        BK.put(bi)
        for i, (gsrc, vi, w) in enumerate([(n1g, 1, 0), (n1g, 1, 1), (n2g, 3, 0)]):
            P.op("dve", lambda e, i=i, gsrc=gsrc, vi=vi, w=w: e.scalar_tensor_tensor(
                gsT[:, i, :], modT[:, vi * 8:(vi + 1) * 8, w], 1.0, gsrc[:], ALU.add, ALU.mult),
                reads=[b_mod, b_const], writes=[b_mod])
        P.dma("sp", "c0", gB_d.rearrange("g p n -> p g n"), gB[:], reads=[b_gB])
        if DEBUG:
            P.dma("sp", "dbg", dbg["modT"], modT[:].rearrange("p a b -> p (a b)"), reads=[b_mod])
            P.dma("sp", "dbg", dbg["g1B"], gB[:, 0, :], reads=[b_gB])
        P.barrier()

    if KSTOP == "S":
        ES.close()
        return nc
    SH1 = lambda kt, w: modT[:, 0 * 8 + kt, w:w + 1]
    SH2 = lambda kt: modT[:, 2 * 8 + kt, 0:1]

    def norm_dma(xt, b_xt, src_rows, ntile, dmakey):
        for t in range(ntile):
            P.dma("sp", dmakey, xt[:, t, :], src_rows(t), writes=[b_xt])

    def norm_stats_units(xt, b_xt, ntile):
        for t in range(ntile):
            P.op("act", lambda e, t=t: e.activation(junk[:], xt[:, t, :], AF.Square, accum_out=stat[:, t:t + 1]),
                 reads=[b_xt, b_junk], writes=[b_stat, b_junk])
            yield
        P.op("act", lambda e: e.activation(stat[:, 4:4 + ntile], stat[:, 0:ntile], AF.Ln, scale=1.0 / D, bias=EPS),
             reads=[b_stat], writes=[b_stat])
        P.op("act", lambda e: e.activation(stat[:, 8:8 + ntile], stat[:, 4:4 + ntile], AF.Exp, scale=-0.5), reads=[b_stat], writes=[b_stat])
        for t in range(ntile):
            P.op("pool", lambda e, t=t: e.tensor_scalar(xt[:, t, :], xt[:, t, :], stat[:, 8 + t:9 + t], 1.0, ALU.mult, ALU.mult),
                 reads=[b_stat, b_xt], writes=[b_xt])
        yield

    def norm_stats(xt, b_xt, ntile):
        for _ in norm_stats_units(xt, b_xt, ntile):
            pass

    def norm_tp(xt, b_xt, hxT, b_hx, ntile, gm_idx, sh_fn, kts=range(8)):
        ncols = 128 * ntile
        for kt in kts:
            bi = BK.get()
            P.mm([lambda e, t=t, kt=kt, bi=bi: e.transpose(BK.t[bi][:, t * 128:(t + 1) * 128], xt[:, t, kt * 128:(kt + 1) * 128], identf[:])
                  for t in range(ntile)], reads=[b_xt, b_const], writes=[BK.buf[bi]])
            if kt % 2 == 0:
                evac(bi, "act", lambda e, kt=kt, bi=bi: e.activation(hxT[:, kt, 0:ncols], BK.t[bi][:, 0:ncols], AF.Identity,
                                                                      scale=gsT[:, gm_idx, kt:kt + 1], bias=sh_fn(kt)),
                     reads=[b_mod], writes=[b_hx])
            else:
                evac(bi, "dve", lambda e, kt=kt, bi=bi: e.tensor_scalar(hxT[:, kt, 0:ncols], BK.t[bi][:, 0:ncols],
                                                                         gsT[:, gm_idx, kt:kt + 1], sh_fn(kt), ALU.mult, ALU.add),
                     reads=[b_mod], writes=[b_hx])
            BK.put(bi)

    def proj_fm(hxT, b_hx, wt, b_wt, col0, ncols, bi):
        P.mm([lambda e, kt=kt: e.matmul(BK.t[bi][:, 0:ncols], wt[:, kt, col0:col0 + 128], hxT[:, kt, 0:ncols],
                                        start=(kt == 0), stop=(kt == 7)) for kt in range(8)],
             reads=[b_hx, b_wt], writes=[BK.buf[bi]])

    with contextlib.ExitStack() as A1:
        kT = sb(A1, "kT", [128, 8, 3, 512], BF16)
        qrT = sb(A1, "qrT", [128, 8, 2, 512], BF16)
        qun = sb(A1, "qun", [128, 4, 2, 512], BF16)
        kun = [sb(A1, "kun%d" % i, [128, 512], BF16) for i in range(2)]
        vaug = sb(A1, "vaug", [128, 12, 4, 3, 64], BF16)
        kcT = sb(A1, "kcT", [128, 8, CTX], BF16)
        vcaug = sb(A1, "vcaug", [128, 2, 4, 3, 64], BF16)
        ropet = sb(A1, "ropet", [128, 2, 512], F32)
        t1 = [sb(A1, "t1_%d" % i, [128, 512], F32) for i in range(2)]
        t2 = [sb(A1, "t2_%d" % i, [128, 512], F32) for i in range(2)]
        NE = 6
        Eb = [sb(A1, "Eb%d" % i, [128, 512], BF16) for i in range(NE)]
        Pb = Eb
        rcp = [sb(A1, "rcp%d" % i, [128, 256], F32) for i in range(2)]
        b_kslot = [Buf("k0"), Buf("k1"), Buf("k2")]
        b_qslot = [Buf("q0"), Buf("q1")]
        b_qun = [[Buf("qun%d_%d" % (i, m)) for m in range(4)] for i in range(2)]
        b_vslot = [Buf("v0"), Buf("v1"), Buf("v2")]
        b_kun = [Buf("kun0"), Buf("kun1")]
        b_rope = Buf("rope")
        b_t1 = [Buf("t1a"), Buf("t1b")]
        b_t2 = [Buf("t2a"), Buf("t2b")]
        b_E = [Buf("E%d" % i) for i in range(NE)]
        b_P = b_E
        b_rcp = [Buf("rcp0"), Buf("rcp1")]
        b_ctx = Buf("ctxkv")

        P.op("pool", lambda e: e.memset(vaug[:].rearrange("p a b c d -> p (a b) c d")[:, :, 1, :], 1.0), writes=b_vslot)
        P.op("pool", lambda e: e.memset(vcaug[:].rearrange("p a b c d -> p (a b) c d")[:, :, 1, :], 1.0), writes=[b_ctx])
        P.op("pool", lambda e: e.memset(kcT[:], 0.0), writes=[b_ctx])
        P.dma("pool", "c1", kT[64:76, :, :, :].rearrange("p h s n -> p h (s n)"), khot_d, writes=b_kslot)

        norm_dma(xt, b_xt, lambda t: ctxb[t * 128:(t + 1) * 128, :], 2, "xl")
        norm_stats(xt, b_xt, 2)
        norm_tp(xt, b_xt, hxT, b_hx, 2, 1, lambda kt: SH1(kt, 1))
        norm_dma(xt, b_xt, xrows(0), 4, "xl")
        for mt in range(4):
            bi = BK.get()
            proj_fm(hxT, b_hx, winA, b_w, 512 + mt * 128, 256, bi)
            for hh in range(2):
                evac(bi, "act", lambda e, mt=mt, bi=bi, hh=hh: e.activation(kcT[hh * 64:(hh + 1) * 64, 2 * mt + hh, :],
                                                                            BK.t[bi][hh * 64:(hh + 1) * 64, 0:256], AF.Copy), writes=[b_ctx])
            BK.put(bi)
        for tt in range(2):
            bi = BK.get()
            P.mm([lambda e, kt=kt, tt=tt, bi=bi: e.matmul(BK.t[bi][:], hxT[:, kt, tt * 128:(tt + 1) * 128], winA[:, kt, 1024:1536],
                                                        start=(kt == 0), stop=(kt == 7)) for kt in range(8)],
                 reads=[b_hx, b_w], writes=[BK.buf[bi]])
            evac(bi, "act", lambda e, tt=tt, bi=bi: e.activation(
                vcaug[:, tt, :, 0:3:2, :], BK.t[bi][:].rearrange("p (a b d) -> p a b d", a=4, b=2), AF.Copy), writes=[b_ctx])
            BK.put(bi)
        norm_stats(xt, b_xt, 4)
        stop_a1 = KSTOP in ("A1a", "A1b", "A1c")

        def proj_units(j, part="all"):
            s = j % 3
            qsl = j % 2
            if part in ("all", "kv"):
                norm_tp(xt, b_xt, hxT, b_hx, 4, 0, lambda kt: SH1(kt, 0), range(0, 4))
                yield
                norm_tp(xt, b_xt, hxT, b_hx, 4, 0, lambda kt: SH1(kt, 0), range(4, 8))
                if j + 1 < NCH:
                    norm_dma(xt, b_xt, xrows(j + 1), 4, "xl")
                P.dma("sp", "rope", ropet[:], rope_d[j], writes=[b_rope])
                yield
            if part in ("all", "q"):
                P.dma("pool", "bm%d" % qsl, qrT[64:76, :, qsl, :], bmask_d[j], writes=[b_qslot[qsl]])
            tiles = [(isk, mt) for isk in range(2) for mt in range(4) if part == "all" or (part == "q") == (isk == 0)]
            st = {}

            def stage_p(n):
                isk, mt = tiles[n]
                bi = BK.get()
                proj_fm(hxT, b_hx, winA, b_w, isk * 512 + mt * 128, 512, bi)
                if isk == 0:
                    src, b_src = qun[:, mt, qsl, :], b_qun[qsl][mt]
                else:
                    src, b_src = kun[n % 2][:], b_kun[n % 2]
                P.op("act", lambda e, src=src, bi=bi: e.activation(src, BK.t[bi][:], AF.Copy), reads=[BK.buf[bi]], writes=[b_src])
                st[n] = (bi, src, b_src)

            def stage_x(n):
                isk, mt = tiles[n]
                bi, src, b_src = st.pop(n)
                bp = BK.get()
                P.mm([lambda e, src=src, bp=bp: e.matmul(BK.t[bp][:], permb[:], src, start=True, stop=True)],
                     reads=[b_src, b_const], writes=[BK.buf[bp]])
                u = n % 2
                P.op("dve", lambda e, u=u, bi=bi: e.tensor_tensor(t1[u][:], BK.t[bi][:], ropet[:, 0, :], ALU.mult),
                     reads=[BK.buf[bi], b_rope, b_src], writes=[b_t1[u]])
                BK.put(bi)
                P.op("dve", lambda e, u=u, bp=bp: e.tensor_tensor(t2[u][:], BK.t[bp][:], ropet[:, 1, :], ALU.mult),
                     reads=[BK.buf[bp], b_rope], writes=[b_t2[u]])
                BK.put(bp)
                if isk == 0:
                    dst, b_dst, dsl = qrT, b_qslot[qsl], qsl
                else:
                    dst, b_dst, dsl = kT, b_kslot[s], s
                for hh in range(2):
                    P.op("pool" if hh == 0 else "dve", lambda e, u=u, hh=hh, dst=dst, mt=mt, dsl=dsl: e.tensor_tensor(
                        dst[0:64, 2 * mt + hh, dsl, :], t1[u][hh * 64:(hh + 1) * 64, :], t2[u][hh * 64:(hh + 1) * 64, :], ALU.add),
                        reads=[b_t1[u], b_t2[u]], writes=[b_dst])

            stage_p(0)
            yield
            for n in range(len(tiles)):
                if n + 1 < len(tiles):
                    stage_p(n + 1)
                stage_x(n)
                yield
            for tt in range(4 if part in ("all", "kv") else 0):
                bi = BK.get()
                P.mm([lambda e, kt=kt, tt=tt, bi=bi: e.matmul(BK.t[bi][:], hxT[:, kt, tt * 128:(tt + 1) * 128], winA[:, kt, 1024:1536],
                                                            start=(kt == 0), stop=(kt == 7)) for kt in range(8)],
                     reads=[b_hx, b_w], writes=[BK.buf[bi]])
                evac(bi, "act", lambda e, tt=tt, bi=bi: e.activation(
                    vaug[:, s * 4 + tt, :, 0:3:2, :], BK.t[bi][:].rearrange("p (a b d) -> p a b d", a=4, b=2), AF.Copy),
                    writes=[b_vslot[s]])
                BK.put(bi)
                yield

        def proj_chunk(j):
            for _ in proj_units(j):
                pass

        def attn_halfgroup(hg, side=None):
            qj = (hg + 1) // 2
            qs = qj % 2
            qo = 256 * ((hg + 1) % 2)
            items = [(hp, pr) for hp in range(4) for pr in range(4)]
            acc_bank = {}

            def issue_scores(idx):
                hp, pr = items[idx]
                for q in range(2):
                    h = 2 * hp + q
                    bi = BK.get()
                    mms = []
                    rd = [b_qslot[qs]]
                    for half in range(2):
                        if pr < 3:
                            t = 2 * pr + half
                            gi = 256 * hg + 128 * t
                            kj, ko = gi // 512, gi % 512
                            ks = kj % 3
                            rd.append(b_kslot[ks])
                            mms.append(lambda e, half=half, ks=ks, ko=ko, h=h, bi=bi: e.matmul(
                                BK.t[bi][:, half * 256:(half + 1) * 256], kT[0:76, h, ks, ko:ko + 128], qrT[0:76, h, qs, qo:qo + 256],
                                start=True, stop=True))
                        else:
                            rd.append(b_ctx)
                            rd.append(b_qun[qs][h // 2])
                            mms.append(lambda e, half=half, h=h, bi=bi: e.matmul(
                                BK.t[bi][:, half * 256:(half + 1) * 256], kcT[:, h, half * 128:(half + 1) * 128],
                                qun[:, h // 2, qs, qo:qo + 256], start=True, stop=True))
                    P.mm(mms, reads=rd, writes=[BK.buf[bi]])
                    eb = (idx % 3) * 2 + q
                    evac(bi, "act", lambda e, bi=bi, eb=eb: e.activation(Eb[eb][:], BK.t[bi][:], AF.Exp, scale=0.125), writes=[b_E[eb]])
                    BK.put(bi)
                    if pr < 3:
                        P.op("dve", lambda e, eb=eb, h=h, pr=pr: e.tensor_tensor(Pb[eb][:], Eb[eb][:], mtab[:, h, pr * 512:(pr + 1) * 512], ALU.mult),
                             reads=[b_E[eb], b_mtab], writes=[b_P[eb]])

            def issue_pv(idx):
                hp, pr = items[idx]
                if pr == 0:
                    acc_bank[hp] = [BK.get(), BK.get()]
                mms = []
                rd = []
                wr = []
                for half in range(2):
                    for q in range(2):
                        h = 2 * hp + q
                        ab = acc_bank[hp][q]
                        eb = (idx % 3) * 2 + q
                        col = q * 64
                        if pr < 3:
                            t = 2 * pr + half
                            gi = 256 * hg + 128 * t
                            kj, ko = gi // 512, gi % 512
                            vt = (kj % 3) * 4 + ko // 128
                            rd.append(b_vslot[kj % 3])
                            lhs = vaug[:, vt, hp, :, :].rearrange("p c d -> p (c d)")[:, col:col + 128]
                            rhs = Pb[eb][:, half * 256:(half + 1) * 256]
                            rd.append(b_P[eb])
                        else:
                            rd.append(b_ctx)
                            lhs = vcaug[:, half, hp, :, :].rearrange("p c d -> p (c d)")[:, col:col + 128]
                            rhs = Eb[eb][:, half * 256:(half + 1) * 256]
                            rd.append(b_E[eb])
                        wr.append(BK.buf[ab])
                        mms.append(lambda e, lhs=lhs, rhs=rhs, ab=ab, first=(pr == 0 and half == 0), last=(pr == 3 and half == 1): e.matmul(
                            BK.t[ab][:, 0:256], lhs, rhs, start=first, stop=last))
                P.mm(mms, reads=rd, writes=wr)
                if pr == 3:
                    for q in range(2):
                        h = 2 * hp + q
                        ab = acc_bank[hp][q]
                        ob = q * 64
                        sbase = 64 - ob
                        r = q
                        if q == 0:
                            P.op("dve", lambda e, ab=ab, ob=ob, sbase=sbase, r=r: e.reciprocal(rcp[r][ob:ob + 64, :], BK.t[ab][sbase:sbase + 64, 0:256]),
                                 reads=[BK.buf[ab]], writes=[b_rcp[r]])
                        else:
                            P.op("act", lambda e, ab=ab, ob=ob, sbase=sbase, r=r: e.activation(rcp[r][ob:ob + 64, :], BK.t[ab][sbase:sbase + 64, 0:256], AF.Ln),
                                 reads=[BK.buf[ab]], writes=[b_rcp[r]])
                            P.op("act", lambda e, ob=ob, r=r: e.activation(rcp[r][ob:ob + 64, :], rcp[r][ob:ob + 64, :], AF.Exp, scale=-1.0),
                                 reads=[b_rcp[r]], writes=[b_rcp[r]])
                        P.op("dve", lambda e, ab=ab, ob=ob, r=r, h=h: e.tensor_tensor(
                            attnT[ob:ob + 64, h // 2, 256 * hg:256 * hg + 256], BK.t[ab][ob:ob + 64, 0:256], rcp[r][ob:ob + 64, :], ALU.mult),
                            reads=[BK.buf[ab], b_rcp[r]], writes=[b_attn[hg]])
                        BK.put(ab)

            LA = 2
            n = len(items)
            for i in range(min(LA, n)):
                issue_scores(i)
            for i in range(n):
                if i + LA < n:
                    issue_scores(i + LA)
                issue_pv(i)
                if side is not None:
                    next(side, None)
            if side is not None:
                for _ in side:
                    pass

        proj_chunk(0)
        norm_stats(xt, b_xt, 4)
        proj_chunk(1)
        def chain(*gens):
            for gen in gens:
                for _ in gen:
                    yield

        for g in range(8):
            attn_halfgroup(2 * g, chain(norm_stats_units(xt, b_xt, 4), proj_units(g + 2, "kv")) if g + 2 < NCH else None)
            attn_halfgroup(2 * g + 1, proj_units(g + 2, "q") if g + 2 < NCH else None)
        P.barrier()

    if DEBUG:
        with contextlib.ExitStack() as DB:
            tmp = sb(DB, "dbgtmp", [128, 4, TOK], F32)
            bt = Buf("dbgtmp")
            P.op("dve", lambda e: e.tensor_copy(tmp[:], attnT[:]), reads=b_attn, writes=[bt])
            P.dma("sp", "dbg", dbg["attnT"], tmp[:], reads=[bt])
            P.barrier()

    print("op counts", P.cnt, {k: v[1] for k, v in P.dma_sems.items()})
    W.close()
    if KSTOP in ("A1", "A1a", "A1b", "A1c"):
        A.close()
        ES.close()
        return nc
    b_xnew = Buf("xnew")
    with contextlib.ExitStack() as A2:
        winB = sb(A2, "winB", [128, 8, 1024], BF16)
        b_wB = Buf("winB")
        wout = sb(A2, "wout", [128, 8, D], BF16)
        b_wout = Buf("w_out")
        for kt in range(0, 8, 4):
            P.dma("pool", "win", winB[:, kt:kt + 4, :], win_v[:, kt:kt + 4, 1536:2560], writes=[b_wB])
        P.dma("pool", "wout", wout[:], w_out.rearrange("(kt p) n -> p kt n", p=128), writes=[b_wout])
        g1B = sb(A2, "g1B", [128, D], F32)
        b_g1 = Buf("g1B")
        P.dma("sp", "c0", g1B[:], gB_d[0], writes=[b_g1])
        dwk_s = sb(A2, "dwk_s", [128, 4, 31], F32)
        cvp_s = sb(A2, "cvp_s", [128, 3, 4], F32)
        b_cv = Buf("cv")
        P.dma("sp", "c0", dwk_s[:], dwk, writes=[b_cv])
        P.dma("sp", "c0", cvp_s[:], cvp, writes=[b_cv])
        HR = 3 * 512 + 272
        hring = sb(A2, "hring", [128, 4, HR], BF16)
        b_hslot = [Buf("h0"), Buf("h1"), Buf("h2"), Buf("hext")]
        diag = sb(A2, "diag", [128, 4, 31, 128], BF16)
        identb = sb(A2, "identb", [128, 128], BF16)
        vms = sb(A2, "vms", [128, 2, 512], BF16)
        sg = [sb(A2, "sg%d" % i, [128, 512], F32) for i in range(2)]
        b_sg = [Buf("sg0"), Buf("sg1")]
        ybuf = sb(A2, "ybuf", [128, 4, 512], F32)
        ysq = sb(A2, "ysq", [128, 4, 512], F32)
        b_y = Buf("y")
        b_ysq = Buf("ysq")
        mstat = sb(A2, "mstat", [128, 4, 512], F32)
        b_ms = Buf("mstat")
        convT = sb(A2, "convT", [128, 2, 4, 512], BF16)
        b_conv = [Buf("convT0"), Buf("convT1")]
        tt_ = [sb(A2, "pt%d" % i, [128, D], F32) for i in range(2)]
        b_tt = [Buf("pt0"), Buf("pt1")]
        xres = [sb(A2, "xres%d" % i, [128, D], F32) for i in range(2)]
        b_xres = [Buf("xres0"), Buf("xres1")]
        b_diag = Buf("diag")
        b_vm = Buf("vm")
        P.dma("pool", "c1", identb[:], ident_d, writes=[b_diag])
        P.dma("pool", "c1", vms[:], vm_d.rearrange("a p n -> p a n"), writes=[b_vm])
        norm_dma(xt, b_xt, xrows(0), 4, "xl")
        for ct in range(4):
            for tap in range(31):
                P.op("dve", lambda e, ct=ct, tap=tap: e.tensor_scalar(diag[:, ct, tap, :], identb[:], dwk_s[:, ct, tap:tap + 1], None, ALU.mult),
                     reads=[b_diag, b_cv], writes=[b_diag])
        norm_stats(xt, b_xt, 4)

        def glu_chunk(j):
            s = j % 3
            norm_tp(xt, b_xt, hxT, b_hx, 4, 0, lambda kt: SH1(kt, 0))
            if j + 1 < NCH:
                norm_dma(xt, b_xt, xrows(j + 1), 4, "xl")
            for ct in range(4):
                ba = BK.get()
                proj_fm(hxT, b_hx, winB, b_wB, ct * 128, 512, ba)
                bb = BK.get()
                proj_fm(hxT, b_hx, winB, b_wB, 512 + ct * 128, 512, bb)
                u = ct % 2
                evac(bb, "act", lambda e, bb=bb, u=u: e.activation(sg[u][:], BK.t[bb][:], AF.Sigmoid), writes=[b_sg[u]])
                BK.put(bb)
                evac(ba, "dve", lambda e, ba=ba, u=u, ct=ct: e.tensor_tensor(hring[:, ct, s * 512:(s + 1) * 512], BK.t[ba][:], sg[u][:], ALU.mult),
                     reads=[b_sg[u]], writes=[b_hslot[s]])
                BK.put(ba)
            if j in (0, NCH - 1):
                w = 0 if j == 0 else 1
                for ct in range(4):
                    P.op("pool", lambda e, w=w, ct=ct: e.tensor_tensor(hring[:, ct, s * 512:(s + 1) * 512], hring[:, ct, s * 512:(s + 1) * 512],
                                                                    vms[:, w, :], ALU.mult),
                         reads=[b_hslot[s], b_vm], writes=[b_hslot[s]])
            if s == 0:
                P.op("pool", lambda e: e.tensor_copy(hring[:, :, 1536:1536 + 272], hring[:, :, 0:272]), reads=[b_hslot[0]], writes=[b_hslot[3]])

        def conv_group(g):
            s = g % 3
            base = s * 512 + 241
            rd = [b_hslot[s], b_hslot[s + 1] if s < 2 else b_hslot[3], b_diag]
            for ct in range(4):
                bi = BK.get()
                P.mm([lambda e, tap=tap, ct=ct, bi=bi: e.matmul(BK.t[bi][:], diag[:, ct, tap, :], hring[:, ct, base + tap:base + tap + 512],
                                                               start=(tap == 0), stop=(tap == 30)) for tap in range(31)],
                     reads=rd, writes=[BK.buf[bi]])
                P.op("act", lambda e, ct=ct, bi=bi: e.activation(ybuf[:, ct, :], BK.t[bi][:], AF.Identity, bias=cvp_s[:, 0, ct:ct + 1]),
                     reads=[BK.buf[bi], b_cv], writes=[b_y])
                evac(bi, "act", lambda e, ct=ct, bi=bi: e.activation(ysq[:, ct, :], BK.t[bi][:], AF.Square, bias=cvp_s[:, 0, ct:ct + 1]),
                     reads=[b_cv], writes=[b_ysq])
                BK.put(bi)
            bs = BK.get()
            P.mm([lambda e, ct=ct: e.matmul(BK.t[bs][:], onesf[:], ybuf[:, ct, :], start=(ct == 0), stop=(ct == 3)) for ct in range(4)],
                 reads=[b_y, b_const], writes=[BK.buf[bs]])
            bq = BK.get()
            P.mm([lambda e, ct=ct: e.matmul(BK.t[bq][:], onesf[:], ysq[:, ct, :], start=(ct == 0), stop=(ct == 3)) for ct in range(4)],
                 reads=[b_ysq, b_const], writes=[BK.buf[bq]])
            P.op("act", lambda e: e.activation(mstat[:, 0, :], BK.t[bs][:], AF.Copy, scale=1.0 / 512), reads=[BK.buf[bs]], writes=[b_ms])
            evac(bs, "act", lambda e: e.activation(mstat[:, 1, :], BK.t[bs][:], AF.Square, scale=1.0 / 512), writes=[b_ms])
            BK.put(bs)
            evac(bq, "dve", lambda e: e.scalar_tensor_tensor(mstat[:, 2, :], BK.t[bq][:], 1.0 / 512, mstat[:, 1, :], ALU.mult, ALU.subtract),
                 reads=[b_ms], writes=[b_ms])
            BK.put(bq)
            P.op("act", lambda e: e.activation(mstat[:, 2, :], mstat[:, 2, :], AF.Ln, bias=EPS), reads=[b_ms], writes=[b_ms])
            P.op("act", lambda e: e.activation(mstat[:, 3, :], mstat[:, 2, :], AF.Exp, scale=-0.5), reads=[b_ms], writes=[b_ms])

        def conv_group_b(g):
            for ct in range(4):
                eng = "dve" if ct != 3 else "pool"
                P.op(eng, lambda e, ct=ct: e.tensor_tensor(ybuf[:, ct, :], ybuf[:, ct, :], mstat[:, 0, :], ALU.subtract),
                     reads=[b_y, b_ms], writes=[b_y])
                P.op(eng, lambda e, ct=ct: e.tensor_tensor(ybuf[:, ct, :], ybuf[:, ct, :], mstat[:, 3, :], ALU.mult),
                     reads=[b_y, b_ms], writes=[b_y])
                P.op("act", lambda e, ct=ct: e.activation(convT[:, g % 2, ct, :], ybuf[:, ct, :], AF.Silu, scale=cvp_s[:, 1, ct:ct + 1],
                                                          bias=cvp_s[:, 2, ct:ct + 1]),
                     reads=[b_y, b_cv], writes=[b_conv[g % 2]])
            if DEBUG and g == 0:
                P.dma("sp", "dbg", dbg["mstat"], mstat[:], reads=[b_ms])
                P.dma("sp", "dbg", dbg["ysq"], ysq[:], reads=[b_ysq])
                P.dma("sp", "dbg", dbg["hring"], hring[:], reads=b_hslot)
                P.barrier()
                P.op("dve", lambda e: e.tensor_copy(ysq[:], convT[:, 0]), reads=[b_conv[0]], writes=[b_ysq])
                P.dma("sp", "dbg", dbg["convT"], ysq[:], reads=[b_ysq])

        def outproj_group(g):
            for tt in range(4):
                u = tt % 2
                tok0 = 512 * g + tt * 128
                P.dma("sp", "xres%d" % u, xres[u][:], xe[HALO + tok0:HALO + tok0 + 128, :], writes=[b_xres[u]])
                for nh in range(2):
                    bi = BK.get()
                    mms = []
                    for kt in range(8):
                        if kt < 4:
                            lhs = attnT[:, kt, tok0:tok0 + 128]
                        else:
                            lhs = convT[:, g % 2, kt - 4, tt * 128:(tt + 1) * 128]
                        mms.append(lambda e, lhs=lhs, kt=kt, nh=nh, bi=bi: e.matmul(BK.t[bi][:], lhs, wout[:, kt, nh * 512:(nh + 1) * 512],
                                                                                 start=(kt == 0), stop=(kt == 7)))
                    P.mm(mms, reads=[b_conv[g % 2], b_wout] + b_attn, writes=[BK.buf[bi]])
                    evac(bi, "dve", lambda e, bi=bi, nh=nh, u=u: e.tensor_tensor(tt_[u][:, nh * 512:(nh + 1) * 512], BK.t[bi][:],
                                                                              g1B[:, nh * 512:(nh + 1) * 512], ALU.mult),
                         reads=[b_g1], writes=[b_tt[u]])
                    BK.put(bi)
                P.op("pool", lambda e, u=u: e.tensor_tensor(xres[u][:], tt_[u][:], xres[u][:], ALU.add),
                     reads=[b_tt[u], b_xres[u]], writes=[b_xres[u]])
                P.dma("sp", "xst%d" % u, xnew_d[tok0:tok0 + 128, :], xres[u][:], reads=[b_xres[u]], writes=[b_xnew])
                if DEBUG:
                    P.dma("sp", "dbg", dbg["xnew"][tok0:tok0 + 128, :], xres[u][:], reads=[b_xres[u]])

        glu_chunk(0)
        norm_stats(xt, b_xt, 4)
        glu_chunk(1)
        norm_stats(xt, b_xt, 4)
        for g in range(8):
            conv_group(g)
            if g + 2 < NCH:
                glu_chunk(g + 2)
            conv_group_b(g)
            if g >= 1:
                outproj_group(g - 1)
            if g + 3 < NCH:
                norm_stats(xt, b_xt, 4)
        outproj_group(7)
        P.barrier()
    A.close()
    if KSTOP == "A2":
        ES.close()
        return nc

    with contextlib.ExitStack() as B:
        CB = 256
        NCB = TOK // CB
        w1s = sb(B, "w1s", [128, 8, 4 * D], BF16)
        w2s = sb(B, "w2s", [128, 32, D], BF16)
        b_w1 = Buf("w1")
        b_w2 = Buf("w2")
        w1_v = w1.rearrange("(kt p) n -> p kt n", p=128)
        w2_v = w2.rearrange("(kt p) n -> p kt n", p=128)
        for kt in range(0, 8, 2):
            P.dma("pool", "w1", w1s[:, kt:kt + 2, :], w1_v[:, kt:kt + 2, :], writes=[b_w1])
        for kt in range(0, 32, 8):
            P.dma("pool", "w2", w2s[:, kt:kt + 8, :], w2_v[:, kt:kt + 8, :], writes=[b_w2])
        xtB = sb(B, "xtB", [128, 2, D], F32)
        b_xtB = Buf("xtB")
        hx2T = sb(B, "hx2T", [128, 8, CB], BF16)
        b_hx2 = Buf("hx2")
        hT = sb(B, "hT", [128, 32, CB], BF16)
        b_hT = Buf("hT")
        fgB = sb(B, "fgBs", [128, D], F32)
        g2B = sb(B, "g2B", [128, D], F32)
        b_fg = Buf("fg")
        sq = [sb(B, "sq%d" % i, [128, CB], F32) for i in range(2)]
        b_sq = [Buf("sq0"), Buf("sq1")]
        xr = [sb(B, "xr%d" % i, [128, D], F32) for i in range(2)]
        b_xr = [Buf("xr0"), Buf("xr1")]
        pt = [sb(B, "ptB%d" % i, [128, D], F32) for i in range(2)]
        b_pt = [Buf("ptB0"), Buf("ptB1")]
        ot = [sb(B, "ot%d" % i, [128, D], F32) for i in range(2)]
        b_ot = [Buf("ot0"), Buf("ot1")]
        fst = sb(B, "fst", [128, 8], F32)
        b_fst = Buf("fst")
        P.dma("sp", "c0", fgB[:], fgB_d, writes=[b_fg])
        P.dma("sp", "c0", g2B[:], gB_d[1], writes=[b_fg])
        b_outd = Buf("outd")
        nrows = lambda c: (lambda t: xnew_d[c * CB + t * 128:c * CB + (t + 1) * 128, :])

        def mlp1(c):
            for fp in range(16):
                bks = [BK.get(), BK.get()]
                mms = []
                for kt in range(8):
                    for q in range(2):
                        ft = 2 * fp + q
                        mms.append(lambda e, kt=kt, ft=ft, bi=bks[q]: e.matmul(BK.t[bi][:, 0:CB], w1s[:, kt, ft * 128:(ft + 1) * 128], hx2T[:, kt, :],
                                                                               start=(kt == 0), stop=(kt == 7)))
                P.mm(mms, reads=[b_hx2, b_w1], writes=[BK.buf[bks[0]], BK.buf[bks[1]]])
                for q in range(2):
                    ft = 2 * fp + q
                    bi = bks[q]
                    u = ft % 2
                    P.op("act", lambda e, bi=bi, u=u: e.activation(sq[u][:], BK.t[bi][:, 0:CB], AF.Square), reads=[BK.buf[bi]], writes=[b_sq[u]])
                    evac(bi, "dve", lambda e, bi=bi, u=u, ft=ft: e.scalar_tensor_tensor(hT[:, ft, :], BK.t[bi][:, 0:CB], 0.0, sq[u][:], ALU.is_gt, ALU.mult),
                         reads=[b_sq[u]], writes=[b_hT])
                    BK.put(bi)

        def mlp2(c):
            for tt in range(CB // 128):
                u = tt % 2
                tok0 = c * CB + tt * 128
                P.dma("sp", "xr%d" % u, xr[u][:], xnew_d[tok0:tok0 + 128, :], writes=[b_xr[u]])
                for nh in range(2):
                    bi = BK.get()
                    P.mm([lambda e, ft=ft, tt=tt, nh=nh, bi=bi: e.matmul(BK.t[bi][:], hT[:, ft, tt * 128:(tt + 1) * 128],
                                                                     w2s[:, ft, nh * 512:(nh + 1) * 512], start=(ft == 0), stop=(ft == 31))
                          for ft in range(32)], reads=[b_hT, b_w2], writes=[BK.buf[bi]])
                    evac(bi, "dve", lambda e, bi=bi, nh=nh, u=u: e.tensor_tensor(pt[u][:, nh * 512:(nh + 1) * 512], BK.t[bi][:],
                                                                              g2B[:, nh * 512:(nh + 1) * 512], ALU.mult),
                         reads=[b_fg], writes=[b_pt[u]])
                    BK.put(bi)
                P.op("pool", lambda e, u=u: e.tensor_tensor(pt[u][:], pt[u][:], xr[u][:], ALU.add),
                     reads=[b_pt[u], b_xr[u]], writes=[b_pt[u]])
                P.op("act", lambda e, u=u: e.activation(junk[:], pt[u][:], AF.Square, accum_out=fst[:, u:u + 1]),
                     reads=[b_pt[u], b_junk], writes=[b_fst, b_junk])
                P.op("act", lambda e, u=u: e.activation(fst[:, 2 + u:3 + u], fst[:, u:u + 1], AF.Ln, scale=1.0 / D, bias=EPS),
                     reads=[b_fst], writes=[b_fst])
                P.op("act", lambda e, u=u: e.activation(fst[:, 4 + u:5 + u], fst[:, 2 + u:3 + u], AF.Exp, scale=-0.5), reads=[b_fst], writes=[b_fst])
                P.op("dve", lambda e, u=u: e.scalar_tensor_tensor(ot[u][:], pt[u][:], fst[:, 4 + u:5 + u], fgB[:], ALU.mult, ALU.mult),
                     reads=[b_pt[u], b_fst, b_fg], writes=[b_ot[u]])
                P.dma("sp", "ost%d" % u, out_d[tok0:tok0 + 128, :], ot[u][:], reads=[b_ot[u]], writes=[b_outd])

        norm_dma(xtB, b_xtB, nrows(0), 2, "xlB")
        norm_stats(xtB, b_xtB, 2)
        norm_tp(xtB, b_xtB, hx2T, b_hx2, 2, 2, lambda kt: SH2(kt))
        norm_dma(xtB, b_xtB, nrows(1), 2, "xlB")
        for c in range(NCB):
            if c + 1 < NCB:
                norm_stats(xtB, b_xtB, 2)
            mlp1(c)
            if c + 1 < NCB:
                norm_tp(xtB, b_xtB, hx2T, b_hx2, 2, 2, lambda kt: SH2(kt))
                if c + 2 < NCB:
                    norm_dma(xtB, b_xtB, nrows(c + 2), 2, "xlB")
            mlp2(c)
        P.barrier()
    ES.close()
    print("op counts", P.cnt, {k: v[1] for k, v in P.dma_sems.items()})
    return nc


def _consts(half):
    row0 = half * 64
    ti = np.arange(TE)
    gt = half * TOK + ti - HALO
    grow = np.floor_divide(gt, 64).astype(np.float32)
    gcol = np.mod(gt, 64).astype(np.float32)
    inv = (np.float32(10000.0) ** (-np.arange(16, dtype=np.float32) / np.float32(16))).astype(np.float32)
    rope = np.zeros((128, 2, TE), np.float32)
    for p in range(128):
        d = p % 64
        pos = grow if d < 32 else gcol
        ang = (pos * inv[d % 16]).astype(np.float32)
        rope[p, 0] = np.cos(ang)
        sn = np.sin(ang)
        rope[p, 1] = -sn if (d % 32) < 16 else sn
    rope = np.ascontiguousarray(rope.reshape(128, 2, NCH, 512).transpose(2, 0, 1, 3))
    perm = np.zeros((128, 128), np.float32)
    for m in range(128):
        k = m + 16 if (m % 32) < 16 else m - 16
        perm[k, m] = 1.0
    khot = np.zeros((12, 8, 3, 512), np.float32)
    for s in range(3):
        for i in range(512):
            khot[(8 * s - 4 + i // 64) % 12, :, s, i] = 1.0
    khot = khot.reshape(12, 8, 1536)
    bmask = np.zeros((NCH, 12, 8, 512), np.float32)
    for j in range(NCH):
        for rr in range(8):
            tau = -HALO + 512 * j + 64 * rr
            if tau < 0 or tau >= TOK:
                continue
            lr = tau // 64
            r0 = 4 * (lr // 4)
            gr = row0 + lr
            rs = min(max(gr - 4, 0), 120)
            for kr in range(r0 - 4, r0 + 8):
                gk = row0 + kr
                ok = rs <= gk < rs + 8
                bmask[j, kr % 12, :, rr * 64:(rr + 1) * 64] = 0.0 if ok else NEG
    vm = np.ones((2, 128, 512), np.float32)
    g0 = half * TOK - HALO + np.arange(512)
    vm[0, :, :] = (g0 >= 0).astype(np.float32)[None, :]
    g8 = half * TOK - HALO + 512 * 8 + np.arange(512)
    vm[1, :, :] = (g8 < SEQ).astype(np.float32)[None, :]
    return rope, perm, khot, bmask, vm


def _mtab(rpb):
    tab = np.full((128, 8, 6, 4, 64), NEG, np.float32)
    qc = np.arange(64)
    cs = np.clip(qc - 8, 0, 48)
    for t in range(6):
        for i in range(2):
            for rq in range(4):
                dl = -4 + 2 * t + i - rq
                if dl < -7 or dl > 7:
                    continue
                for kc in range(64):
                    ok = (kc >= cs) & (kc < cs + 16)
                    co = np.clip(kc - qc + 15, 0, 30)
                    vals = rpb[:, dl + 7, :][:, co]
                    tab[i * 64 + kc, :, t, rq, :] = np.where(ok[None, :], vals, NEG)
    return tab.reshape(128, 8, 1536)


_CACHE = {}


def kernel(x, c, ctx, c_ctx, w_ada, b_ada, norm1_g, norm2_g, w_in, dw_kernel, dw_bias,
           conv_ln_g, conv_ln_b, rpb, w_out, w_mlp1, w_mlp2, final_g):
    f = lambda a: np.ascontiguousarray(np.asarray(a, dtype=np.float32))
    x, c, ctx, c_ctx = f(x), f(c), f(ctx), f(c_ctx)
    w_ada0, b_ada0 = f(w_ada)[0], f(b_ada)[0]
    fm = lambda v, n: np.ascontiguousarray(v.reshape(n, 128).T)
    shared = {
        "w_ada": w_ada0,
        "b_adaT": fm(b_ada0, 48),
        "b_ada_g": np.ascontiguousarray(np.concatenate([b_ada0[2 * D:3 * D], b_ada0[5 * D:6 * D]])[None, :]),
        "n1gT": fm(f(norm1_g)[0], 8),
        "n2gT": fm(f(norm2_g)[0], 8),
        "fgB": np.ascontiguousarray(np.broadcast_to(f(final_g)[None, :], (128, D))),
        "w_in": f(w_in)[0],
        "w_out": f(w_out)[0],
        "w1": f(w_mlp1)[0],
        "w2": f(w_mlp2)[0],
        "dwk": np.ascontiguousarray(f(dw_kernel)[0].reshape(31, 4, 128).transpose(2, 1, 0)),
        "cvp": np.ascontiguousarray(np.stack([fm(f(dw_bias)[0], 4), fm(f(conv_ln_g)[0], 4), fm(f(conv_ln_b)[0], 4)], axis=1)),
        "ident": np.eye(128, dtype=np.float32),
        "mtab": _mtab(f(rpb)[0]),
    }
    in_maps = []
    for core in range(8):
        b, half = core // 2, core % 2
        rope, perm, khot, bmask, vm = _consts(half)
        xe = np.zeros((TE, D), np.float32)
        lo = half * TOK - HALO
        a0, a1 = max(lo, 0), min(lo + TE, SEQ)
        xe[a0 - lo:a1 - lo] = x[b, a0:a1]
        cv = np.stack([fm(c[b], 8), fm(c_ctx, 8)], axis=2)
        m = dict(shared)
        m.update({"xe": xe, "ctxb": np.ascontiguousarray(ctx[b]), "cvec": np.ascontiguousarray(cv), "perm": perm,
                  "rope": rope, "khot": khot, "bmask": bmask, "vm": vm})
        in_maps.append(m)
    if "nc" not in _CACHE:
        _CACHE["nc"] = build_nc()
    ncore = int(os.environ.get("KCORES", "8"))
    res = run_bass_kernel_spmd(_CACHE["nc"], in_maps[:ncore], core_ids=list(range(ncore)))
    out = np.zeros((NB, SEQ, D), np.float32)
    for core in range(ncore):
        b, half = core // 2, core % 2
        out[b, half * TOK:(half + 1) * TOK] = res.results[core]["out"]
    if DEBUG:
        kernel.dbg = res.results
    return out
```

```python
import contextlib
import os
import numpy as np
import concourse.bass as bass
import concourse.mybir as mybir
from concourse.bass_utils import run_bass_kernel_spmd

F32 = mybir.dt.float32
BF16 = mybir.dt.bfloat16
AF = mybir.ActivationFunctionType
ALU = mybir.AluOpType

D = 1024
SEQ = 8192
NB = 4
TOK = 4096
HALO = 256
TE = TOK + 2 * HALO
NCH = 9
CTX = 256
EPS = 1e-6
NEG = -30000.0
DEBUG = bool(int(os.environ.get("KDEBUG", "0")))
KSTOP = os.environ.get("KSTOP", "")
KSUB = int(os.environ.get("KSUB", "9"))


class Buf:
    def __init__(self, name=""):
        self.name = name
        self.ws = {}
        self.rs = {}


class Prog:
    def __init__(self, nc):
        self.nc = nc
        self.engs = {"pe": nc.tensor, "act": nc.scalar, "dve": nc.vector, "pool": nc.gpsimd, "sp": nc.sync}
        self.sems = {k: nc.alloc_semaphore("sem_" + k) for k in self.engs}
        self.cnt = {k: 0 for k in self.engs}
        self.waited = {k: {} for k in self.engs}
        self.dma_sems = {}

    def _wait(self, e, deps):
        for key, (sem, val) in deps.items():
            if self.waited[e].get(key, 0) < val:
                self.engs[e].wait_ge(sem, val)
                self.waited[e][key] = val

    def _deps(self, reads, writes):
        deps = {}
        for b in reads:
            for k, v in b.ws.items():
                if k not in deps or deps[k][1] < v[1]:
                    deps[k] = v
        for b in writes:
            for k, v in b.rs.items():
                if k not in deps or deps[k][1] < v[1]:
                    deps[k] = v
        return deps

    def _mark(self, key, tok, reads, writes):
        for b in writes:
            if b.rs:
                b.ws = {}
                b.rs = {}
            b.ws[key] = tok
        for b in reads:
            b.rs[key] = tok

    def op(self, e, fn, reads=(), writes=()):
        self._wait(e, self._deps(reads, writes))
        inst = fn(self.engs[e])
        self.cnt[e] += 1
        inst.then_inc(self.sems[e], 1)
        self._mark(e, (self.sems[e], self.cnt[e]), reads, writes)

    def mm(self, mms, reads=(), writes=()):
        self._wait("pe", self._deps(reads, writes))
        inst = None
        for fn in mms:
            inst = fn(self.engs["pe"])
        self.cnt["pe"] += 1
        inst.then_inc(self.sems["pe"], 1)
        self._mark("pe", (self.sems["pe"], self.cnt["pe"]), reads, writes)

    def dma(self, e, key, out, in_, reads=(), writes=()):
        if key in ("c0", "c1"):
            self.nuniq = getattr(self, "nuniq", 0) + 1
            key = "u%d" % self.nuniq
        self._wait(e, self._deps(reads, writes))
        if key not in self.dma_sems:
            self.dma_sems[key] = [self.nc.alloc_semaphore("sd_" + key), 0]
        s = self.dma_sems[key]
        s[1] += 16
        self.engs[e].dma_start(out=out, in_=in_).then_inc(s[0], 16)
        self._mark("dma_" + key, (s[0], s[1]), reads, writes)

    def barrier(self):
        deps = {k: (self.sems[k], self.cnt[k]) for k in self.engs if self.cnt[k] > 0}
        for key, s in self.dma_sems.items():
            deps["dma_" + key] = (s[0], s[1])
        for e in self.engs:
            self._wait(e, deps)


class Banks:
    def __init__(self, nc):
        self.t = [nc.alloc_psum_tensor("bank%d" % i, [128, 512], F32) for i in range(8)]
        self.buf = [Buf("bank%d" % i) for i in range(8)]
        self.free = list(range(8))

    def get(self):
        assert self.free, "out of PSUM banks"
        i = self.free.pop(0)
        return i

    def put(self, i):
        self.free.append(i)


def build_nc():
    nc = bass.Bass("TRN2", target_bir_lowering=False)
    P = Prog(nc)
    BK = Banks(nc)

    def din(name, shape, dt=F32):
        return nc.dram_tensor(name, list(shape), dt, kind="ExternalInput").ap()

    xe = din("xe", [TE, D])
    ctxb = din("ctxb", [CTX, D])
    cvec = din("cvec", [128, 8, 2])
    w_ada = din("w_ada", [D, 6 * D])
    b_adaT = din("b_adaT", [128, 48])
    b_ada_g = din("b_ada_g", [1, 2 * D])
    n1gT = din("n1gT", [128, 8])
    n2gT = din("n2gT", [128, 8])
    fgB_d = din("fgB", [128, D])
    w_in = din("w_in", [D, 2560])
    w_out = din("w_out", [D, D])
    w1 = din("w1", [D, 4 * D])
    w2 = din("w2", [4 * D, D])
    dwk = din("dwk", [128, 4, 31])
    cvp = din("cvp", [128, 3, 4])
    ident_d = din("ident", [128, 128])
    perm_d = din("perm", [128, 128])
    rope_d = din("rope", [NCH, 128, 2, 512])
    mtab_d = din("mtab", [128, 8, 1536])
    khot_d = din("khot", [12, 8, 1536])
    bmask_d = din("bmask", [NCH, 12, 8, 512])
    vm_d = din("vm", [2, 128, 512])
    out_d = nc.dram_tensor("out", [TOK, D], F32, kind="ExternalOutput").ap()
    xnew_d = nc.dram_tensor("xnew", [TOK, D], F32).ap()
    dbg = {}
    if DEBUG:
        dbg["attnT"] = nc.dram_tensor("dbg_attnT", [128, 4, TOK], F32, kind="ExternalOutput").ap()
        dbg["modT"] = nc.dram_tensor("dbg_modT", [128, 64], F32, kind="ExternalOutput").ap()
        dbg["g1B"] = nc.dram_tensor("dbg_g1B", [128, D], F32, kind="ExternalOutput").ap()
        dbg["hx0"] = nc.dram_tensor("dbg_hx0", [128, 8, 512], F32, kind="ExternalOutput").ap()
        dbg["xnew"] = nc.dram_tensor("dbg_xnew", [TOK, D], F32, kind="ExternalOutput").ap()
        dbg["convT"] = nc.dram_tensor("dbg_convT", [128, 4, 512], F32, kind="ExternalOutput").ap()
        dbg["q0"] = nc.dram_tensor("dbg_q0", [128, 8, 512], F32, kind="ExternalOutput").ap()
        dbg["mstat"] = nc.dram_tensor("dbg_mstat", [128, 4, 512], F32, kind="ExternalOutput").ap()
        dbg["ysq"] = nc.dram_tensor("dbg_ysq", [128, 4, 512], F32, kind="ExternalOutput").ap()
        dbg["hring"] = nc.dram_tensor("dbg_hring", [128, 4, 1808], BF16, kind="ExternalOutput").ap()

    ES = contextlib.ExitStack()

    def sb(stack, name, shape, dt):
        return stack.enter_context(nc.sbuf_tensor("s_" + name, list(shape), dt))

    identf = sb(ES, "identf", [128, 128], F32)
    onesf = sb(ES, "onesf", [128, 128], F32)
    onesb = sb(ES, "onesb", [128, 128], BF16)
    permb = sb(ES, "permb", [128, 128], BF16)
    modT = sb(ES, "modT", [128, 32, 2], F32)
    gsT = sb(ES, "gsT", [128, 3, 8], F32)
    n1g = sb(ES, "n1g", [128, 8], F32)
    n2g = sb(ES, "n2g", [128, 8], F32)
    stat = sb(ES, "stat", [128, 16], F32)
    junk = sb(ES, "junk", [128, D], BF16)
    gB_d = nc.dram_tensor("gB_scr", [2, 128, D], F32).ap()
    b_const = Buf("const")
    b_mod = Buf("mod")
    b_stat = Buf("stat")
    b_junk = Buf("junk")

    P.dma("sp", "c0", identf[:], ident_d, writes=[b_const])
    P.dma("pool", "c1", permb[:], perm_d, writes=[b_const])
    P.dma("sp", "c0", n1g[:], n1gT, writes=[b_const])
    P.dma("sp", "c0", n2g[:], n2gT, writes=[b_const])
    P.op("dve", lambda e: e.memset(onesf[:], 1.0), writes=[b_const])
    P.op("dve", lambda e: e.memset(onesb[:], 1.0), writes=[b_const])

    def evac(i, fn_engine, fn, reads=(), writes=()):
        P.op(fn_engine, fn, reads=[BK.buf[i]] + list(reads), writes=writes)

    A = contextlib.ExitStack()
    attnT = sb(A, "attnT", [128, 4, TOK], BF16)
    xt = sb(A, "xt", [128, 4, D], F32)
    b_xt = Buf("xt")
    hxT = sb(A, "hxT", [128, 8, 512], BF16)
    b_hx = Buf("hx")
    b_attn = [Buf("attn%d" % i) for i in range(16)]
    win_v = w_in.rearrange("(kt p) n -> p kt n", p=128)
    xrows = lambda j: (lambda t: xe[j * 512 + t * 128:j * 512 + (t + 1) * 128, :])

    W = contextlib.ExitStack()
    winA = sb(W, "winA", [128, 8, 1536], BF16)
    b_w = Buf("winA")
    mtab = sb(W, "mtab", [128, 8, 1536], BF16)
    b_mtab = Buf("mtab")
    mt_src = mtab_d.rearrange("p h n -> p (h n)")
    mt_dst = mtab[:].rearrange("p h n -> p (h n)")
    xt_flat = xt[:].rearrange("p t n -> p (t n)")
    for r3 in range(3):
        P.dma("sp", "xl", xt_flat, mt_src[:, r3 * 4096:(r3 + 1) * 4096], writes=[b_xt])
        P.op("act", lambda e, r3=r3: e.activation(mt_dst[:, r3 * 4096:(r3 + 1) * 4096], xt_flat, AF.Exp), reads=[b_xt], writes=[b_mtab])

    with contextlib.ExitStack() as S:
        cv = sb(S, "cv", [128, 8, 2], F32)
        scT = sb(S, "scT", [128, 8, 2], BF16)
        scTf = sb(S, "scTf", [128, 8, 2], F32)
        scB = sb(S, "scB", [128, 8, 128], BF16)
        badaT = sb(S, "badaT", [128, 48], F32)
        badag = sb(S, "badag", [1, 2 * D], BF16)
        gB = sb(S, "gB", [128, 2, D], F32)
        slab = [sb(S, "slab%d" % i, [128, 8, D], BF16) for i in range(2)]
        b_slab = [Buf("slab0"), Buf("slab1")]
        b_s = Buf("s_misc")
        b_gB = Buf("gB")
        P.dma("sp", "c0", cv[:], cvec, writes=[b_s])
        P.dma("sp", "c0", badaT[:], b_adaT, writes=[b_s])
        P.dma("pool", "c1", badag[:], b_ada_g, writes=[b_s])
        wada_v = w_ada.rearrange("(kt p) n -> p kt n", p=128)
        order = [0, 1, 3, 4, 2, 5]
        P.dma("pool", "slab0", slab[0][:], wada_v[:, :, order[0] * D:(order[0] + 1) * D], writes=[b_slab[0]])
        P.dma("pool", "slab1", slab[1][:], wada_v[:, :, order[1] * D:(order[1] + 1) * D], writes=[b_slab[1]])
        PREFETCH_WIN = True
        P.op("act", lambda e: e.activation(scTf[:], cv[:], AF.Silu), reads=[b_s], writes=[b_s])
        P.op("act", lambda e: e.activation(scT[:], cv[:], AF.Silu), reads=[b_s], writes=[b_s])
        for kt in range(8):
            P.op("dve", lambda e, kt=kt: e.tensor_scalar(scB[:, kt, :], onesb[:], scTf[:, kt, 0:1], None, ALU.mult),
                 reads=[b_s, b_const], writes=[b_s])
        bi = BK.get()
        for n, v in enumerate(order):
            s = n % 2
            if v in (0, 1, 3, 4):
                vi = {0: 0, 1: 1, 3: 2, 4: 3}[v]
                mms = []
                for mt in range(8):
                    for kt in range(8):
                        mms.append(lambda e, mt=mt, kt=kt, vi=vi, s=s: e.matmul(
                            BK.t[bi][:, (vi * 8 + mt) * 2:(vi * 8 + mt) * 2 + 2], slab[s][:, kt, mt * 128:(mt + 1) * 128],
                            scT[:, kt, :], start=(kt == 0), stop=(kt == 7)))
                P.mm(mms, reads=[b_slab[s], b_s], writes=[BK.buf[bi]])
            else:
                gi = 0 if v == 2 else 1
                for nh in range(2):
                    bj = BK.get()
                    mms = []
                    for kt in range(8):
                        mms.append(lambda e, kt=kt, s=s, nh=nh, bj=bj: e.matmul(
                            BK.t[bj][:], scB[:, kt, :], slab[s][:, kt, nh * 512:(nh + 1) * 512], start=(kt == 0), stop=False))
                    mms.append(lambda e, nh=nh, bj=bj, gi=gi: e.matmul(
                        BK.t[bj][:], onesb[0:1, :], badag[0:1, gi * D + nh * 512:gi * D + (nh + 1) * 512], start=False, stop=True))
                    P.mm(mms, reads=[b_slab[s], b_s, b_const], writes=[BK.buf[bj]])
                    evac(bj, "act", lambda e, bj=bj, nh=nh, gi=gi: e.activation(gB[:, gi, nh * 512:(nh + 1) * 512], BK.t[bj][:], AF.Copy),
                         writes=[b_gB])
                    BK.put(bj)
            if n + 2 < 6:
                v2 = order[n + 2]
                P.dma("pool", "slab%d" % s, slab[s][:], wada_v[:, :, v2 * D:(v2 + 1) * D], writes=[b_slab[s]])
        for kt in range(0, 8, 2):
            P.dma("pool", "win", winA[:, kt:kt + 2, :], win_v[:, kt:kt + 2, 0:1536], writes=[b_w])
        for vi, v in enumerate([0, 1, 3, 4]):
            for w in range(2):
                evac(bi, "dve", lambda e, vi=vi, v=v, w=w: e.tensor_tensor(
                    modT[:, vi * 8:(vi + 1) * 8, w],
                    BK.t[bi][:, vi * 16:(vi + 1) * 16].rearrange("p (m w) -> p m w", w=2)[:, :, w],
                    badaT[:, v * 8:(v + 1) * 8], ALU.add), reads=[b_s], writes=[b_mod])
        BK.put(bi)
        for i, (gsrc, vi, w) in enumerate([(n1g, 1, 0), (n1g, 1, 1), (n2g, 3, 0)]):
            P.op("dve", lambda e, i=i, gsrc=gsrc, vi=vi, w=w: e.scalar_tensor_tensor(
                gsT[:, i, :], modT[:, vi * 8:(vi + 1) * 8, w], 1.0, gsrc[:], ALU.add, ALU.mult),
                reads=[b_mod, b_const], writes=[b_mod])
        P.dma("sp", "c0", gB_d.rearrange("g p n -> p g n"), gB[:], reads=[b_gB])
        if DEBUG:
            P.dma("sp", "dbg", dbg["modT"], modT[:].rearrange("p a b -> p (a b)"), reads=[b_mod])
            P.dma("sp", "dbg", dbg["g1B"], gB[:, 0, :], reads=[b_gB])
        P.barrier()

    if KSTOP == "S":
        ES.close()
        return nc
    SH1 = lambda kt, w: modT[:, 0 * 8 + kt, w:w + 1]
    SH2 = lambda kt: modT[:, 2 * 8 + kt, 0:1]

    def norm_dma(xt, b_xt, src_rows, ntile, dmakey):
        for t in range(ntile):
            P.dma("sp", dmakey, xt[:, t, :], src_rows(t), writes=[b_xt])

    def norm_stats_units(xt, b_xt, ntile):
        for t in range(ntile):
            P.op("act", lambda e, t=t: e.activation(junk[:], xt[:, t, :], AF.Square, accum_out=stat[:, t:t + 1]),
                 reads=[b_xt, b_junk], writes=[b_stat, b_junk])
            yield
        P.op("act", lambda e: e.activation(stat[:, 4:4 + ntile], stat[:, 0:ntile], AF.Ln, scale=1.0 / D, bias=EPS),
             reads=[b_stat], writes=[b_stat])
        P.op("act", lambda e: e.activation(stat[:, 8:8 + ntile], stat[:, 4:4 + ntile], AF.Exp, scale=-0.5), reads=[b_stat], writes=[b_stat])
        for t in range(ntile):
            P.op("pool", lambda e, t=t: e.tensor_scalar(xt[:, t, :], xt[:, t, :], stat[:, 8 + t:9 + t], 1.0, ALU.mult, ALU.mult),
                 reads=[b_stat, b_xt], writes=[b_xt])
        yield

    def norm_stats(xt, b_xt, ntile):
        for _ in norm_stats_units(xt, b_xt, ntile):
            pass

    def norm_tp(xt, b_xt, hxT, b_hx, ntile, gm_idx, sh_fn, kts=range(8)):
        ncols = 128 * ntile
        for kt in kts:
            bi = BK.get()
            P.mm([lambda e, t=t, kt=kt, bi=bi: e.transpose(BK.t[bi][:, t * 128:(t + 1) * 128], xt[:, t, kt * 128:(kt + 1) * 128], identf[:])
                  for t in range(ntile)], reads=[b_xt, b_const], writes=[BK.buf[bi]])
            if kt % 2 == 0:
                evac(bi, "act", lambda e, kt=kt, bi=bi: e.activation(hxT[:, kt, 0:ncols], BK.t[bi][:, 0:ncols], AF.Identity,
                                                                      scale=gsT[:, gm_idx, kt:kt + 1], bias=sh_fn(kt)),
                     reads=[b_mod], writes=[b_hx])
            else:
                evac(bi, "dve", lambda e, kt=kt, bi=bi: e.tensor_scalar(hxT[:, kt, 0:ncols], BK.t[bi][:, 0:ncols],
                                                                         gsT[:, gm_idx, kt:kt + 1], sh_fn(kt), ALU.mult, ALU.add),
                     reads=[b_mod], writes=[b_hx])
            BK.put(bi)

    def proj_fm(hxT, b_hx, wt, b_wt, col0, ncols, bi):
        P.mm([lambda e, kt=kt: e.matmul(BK.t[bi][:, 0:ncols], wt[:, kt, col0:col0 + 128], hxT[:, kt, 0:ncols],
                                        start=(kt == 0), stop=(kt == 7)) for kt in range(8)],
             reads=[b_hx, b_wt], writes=[BK.buf[bi]])

    with contextlib.ExitStack() as A1:
        kT = sb(A1, "kT", [128, 8, 3, 512], BF16)
        qrT = sb(A1, "qrT", [128, 8, 2, 512], BF16)
        qun = sb(A1, "qun", [128, 4, 2, 512], BF16)
        kun = [sb(A1, "kun%d" % i, [128, 512], BF16) for i in range(2)]
        vaug = sb(A1, "vaug", [128, 12, 4, 3, 64], BF16)
        kcT = sb(A1, "kcT", [128, 8, CTX], BF16)
        vcaug = sb(A1, "vcaug", [128, 2, 4, 3, 64], BF16)
        ropet = sb(A1, "ropet", [128, 2, 512], F32)
        t1 = [sb(A1, "t1_%d" % i, [128, 512], F32) for i in range(2)]
        t2 = [sb(A1, "t2_%d" % i, [128, 512], F32) for i in range(2)]
        NE = 6
        Eb = [sb(A1, "Eb%d" % i, [128, 512], BF16) for i in range(NE)]
        Pb = Eb
        rcp = [sb(A1, "rcp%d" % i, [128, 256], F32) for i in range(2)]
        b_kslot = [Buf("k0"), Buf("k1"), Buf("k2")]
        b_qslot = [Buf("q0"), Buf("q1")]
        b_qun = [[Buf("qun%d_%d" % (i, m)) for m in range(4)] for i in range(2)]
        b_vslot = [Buf("v0"), Buf("v1"), Buf("v2")]
        b_kun = [Buf("kun0"), Buf("kun1")]
        b_rope = Buf("rope")
        b_t1 = [Buf("t1a"), Buf("t1b")]
        b_t2 = [Buf("t2a"), Buf("t2b")]
        b_E = [Buf("E%d" % i) for i in range(NE)]
        b_P = b_E
        b_rcp = [Buf("rcp0"), Buf("rcp1")]
        b_ctx = Buf("ctxkv")

        P.op("pool", lambda e: e.memset(vaug[:].rearrange("p a b c d -> p (a b) c d")[:, :, 1, :], 1.0), writes=b_vslot)
        P.op("pool", lambda e: e.memset(vcaug[:].rearrange("p a b c d -> p (a b) c d")[:, :, 1, :], 1.0), writes=[b_ctx])
        P.op("pool", lambda e: e.memset(kcT[:], 0.0), writes=[b_ctx])
        P.dma("pool", "c1", kT[64:76, :, :, :].rearrange("p h s n -> p h (s n)"), khot_d, writes=b_kslot)

        norm_dma(xt, b_xt, lambda t: ctxb[t * 128:(t + 1) * 128, :], 2, "xl")
        norm_stats(xt, b_xt, 2)
        norm_tp(xt, b_xt, hxT, b_hx, 2, 1, lambda kt: SH1(kt, 1))
        norm_dma(xt, b_xt, xrows(0), 4, "xl")
        for mt in range(4):
            bi = BK.get()
            proj_fm(hxT, b_hx, winA, b_w, 512 + mt * 128, 256, bi)
            for hh in range(2):
                evac(bi, "act", lambda e, mt=mt, bi=bi, hh=hh: e.activation(kcT[hh * 64:(hh + 1) * 64, 2 * mt + hh, :],
                                                                            BK.t[bi][hh * 64:(hh + 1) * 64, 0:256], AF.Copy), writes=[b_ctx])
            BK.put(bi)
        for tt in range(2):
            bi = BK.get()
            P.mm([lambda e, kt=kt, tt=tt, bi=bi: e.matmul(BK.t[bi][:], hxT[:, kt, tt * 128:(tt + 1) * 128], winA[:, kt, 1024:1536],
                                                        start=(kt == 0), stop=(kt == 7)) for kt in range(8)],
                 reads=[b_hx, b_w], writes=[BK.buf[bi]])
            evac(bi, "act", lambda e, tt=tt, bi=bi: e.activation(
                vcaug[:, tt, :, 0:3:2, :], BK.t[bi][:].rearrange("p (a b d) -> p a b d", a=4, b=2), AF.Copy), writes=[b_ctx])
            BK.put(bi)
        norm_stats(xt, b_xt, 4)
        stop_a1 = KSTOP in ("A1a", "A1b", "A1c")

        def proj_units(j):
            s = j % 3
            qsl = j % 2
            norm_tp(xt, b_xt, hxT, b_hx, 4, 0, lambda kt: SH1(kt, 0), range(0, 4))
            yield
            norm_tp(xt, b_xt, hxT, b_hx, 4, 0, lambda kt: SH1(kt, 0), range(4, 8))
            if j + 1 < NCH:
                norm_dma(xt, b_xt, xrows(j + 1), 4, "xl")
            P.dma("sp", "rope", ropet[:], rope_d[j], writes=[b_rope])
            P.dma("pool", "bm%d" % qsl, qrT[64:76, :, qsl, :], bmask_d[j], writes=[b_qslot[qsl]])
            tiles = [(isk, mt) for isk in range(2) for mt in range(4)]
            st = {}

            def stage_p(n):
                isk, mt = tiles[n]
                bi = BK.get()
                proj_fm(hxT, b_hx, winA, b_w, isk * 512 + mt * 128, 512, bi)
                if isk == 0:
                    src, b_src = qun[:, mt, qsl, :], b_qun[qsl][mt]
                else:
                    src, b_src = kun[n % 2][:], b_kun[n % 2]
                P.op("act", lambda e, src=src, bi=bi: e.activation(src, BK.t[bi][:], AF.Copy), reads=[BK.buf[bi]], writes=[b_src])
                st[n] = (bi, src, b_src)

            def stage_x(n):
                isk, mt = tiles[n]
                bi, src, b_src = st.pop(n)
                bp = BK.get()
                P.mm([lambda e, src=src, bp=bp: e.matmul(BK.t[bp][:], permb[:], src, start=True, stop=True)],
                     reads=[b_src, b_const], writes=[BK.buf[bp]])
                u = n % 2
                P.op("dve", lambda e, u=u, bi=bi: e.tensor_tensor(t1[u][:], BK.t[bi][:], ropet[:, 0, :], ALU.mult),
                     reads=[BK.buf[bi], b_rope, b_src], writes=[b_t1[u]])
                BK.put(bi)
                P.op("dve", lambda e, u=u, bp=bp: e.tensor_tensor(t2[u][:], BK.t[bp][:], ropet[:, 1, :], ALU.mult),
                     reads=[BK.buf[bp], b_rope], writes=[b_t2[u]])
                BK.put(bp)
                if isk == 0:
                    dst, b_dst, dsl = qrT, b_qslot[qsl], qsl
                else:
                    dst, b_dst, dsl = kT, b_kslot[s], s
                for hh in range(2):
                    P.op("pool" if hh == 0 else "dve", lambda e, u=u, hh=hh, dst=dst, mt=mt, dsl=dsl: e.tensor_tensor(
                        dst[0:64, 2 * mt + hh, dsl, :], t1[u][hh * 64:(hh + 1) * 64, :], t2[u][hh * 64:(hh + 1) * 64, :], ALU.add),
                        reads=[b_t1[u], b_t2[u]], writes=[b_dst])

            stage_p(0)
            yield
            for n in range(8):
                if n + 1 < 8:
                    stage_p(n + 1)
                stage_x(n)
                yield
            for tt in range(4 if KSUB >= 3 else 0):
                bi = BK.get()
                P.mm([lambda e, kt=kt, tt=tt, bi=bi: e.matmul(BK.t[bi][:], hxT[:, kt, tt * 128:(tt + 1) * 128], winA[:, kt, 1024:1536],
                                                            start=(kt == 0), stop=(kt == 7)) for kt in range(8)],
                     reads=[b_hx, b_w], writes=[BK.buf[bi]])
                evac(bi, "act", lambda e, tt=tt, bi=bi: e.activation(
                    vaug[:, s * 4 + tt, :, 0:3:2, :], BK.t[bi][:].rearrange("p (a b d) -> p a b d", a=4, b=2), AF.Copy),
                    writes=[b_vslot[s]])
                BK.put(bi)
                yield

        def proj_chunk(j):
            for _ in proj_units(j):
                pass

        def attn_halfgroup(hg, side=None):
            qj = (hg + 1) // 2
            qs = qj % 2
            qo = 256 * ((hg + 1) % 2)
            items = [(hp, pr) for hp in range(4) for pr in range(4)]
            acc_bank = {}

            def issue_scores(idx):
                hp, pr = items[idx]
                for q in range(2):
                    h = 2 * hp + q
                    bi = BK.get()
                    mms = []
                    rd = [b_qslot[qs]]
                    for half in range(2):
                        if pr < 3:
                            t = 2 * pr + half
                            gi = 256 * hg + 128 * t
                            kj, ko = gi // 512, gi % 512
                            ks = kj % 3
                            rd.append(b_kslot[ks])
                            mms.append(lambda e, half=half, ks=ks, ko=ko, h=h, bi=bi: e.matmul(
                                BK.t[bi][:, half * 256:(half + 1) * 256], kT[0:76, h, ks, ko:ko + 128], qrT[0:76, h, qs, qo:qo + 256],
                                start=True, stop=True))
                        else:
                            rd.append(b_ctx)
                            rd.append(b_qun[qs][h // 2])
                            mms.append(lambda e, half=half, h=h, bi=bi: e.matmul(
                                BK.t[bi][:, half * 256:(half + 1) * 256], kcT[:, h, half * 128:(half + 1) * 128],
                                qun[:, h // 2, qs, qo:qo + 256], start=True, stop=True))
                    P.mm(mms, reads=rd, writes=[BK.buf[bi]])
                    eb = (idx % 3) * 2 + q
                    evac(bi, "act", lambda e, bi=bi, eb=eb: e.activation(Eb[eb][:], BK.t[bi][:], AF.Exp, scale=0.125), writes=[b_E[eb]])
                    BK.put(bi)
                    if pr < 3:
                        P.op("dve", lambda e, eb=eb, h=h, pr=pr: e.tensor_tensor(Pb[eb][:], Eb[eb][:], mtab[:, h, pr * 512:(pr + 1) * 512], ALU.mult),
                             reads=[b_E[eb], b_mtab], writes=[b_P[eb]])

            def issue_pv(idx):
                hp, pr = items[idx]
                if pr == 0:
                    acc_bank[hp] = [BK.get(), BK.get()]
                mms = []
                rd = []
                wr = []
                for half in range(2):
                    for q in range(2):
                        h = 2 * hp + q
                        ab = acc_bank[hp][q]
                        eb = (idx % 3) * 2 + q
                        col = q * 64
                        if pr < 3:
                            t = 2 * pr + half
                            gi = 256 * hg + 128 * t
                            kj, ko = gi // 512, gi % 512
                            vt = (kj % 3) * 4 + ko // 128
                            rd.append(b_vslot[kj % 3])
                            lhs = vaug[:, vt, hp, :, :].rearrange("p c d -> p (c d)")[:, col:col + 128]
                            rhs = Pb[eb][:, half * 256:(half + 1) * 256]
                            rd.append(b_P[eb])
                        else:
                            rd.append(b_ctx)
                            lhs = vcaug[:, half, hp, :, :].rearrange("p c d -> p (c d)")[:, col:col + 128]
                            rhs = Eb[eb][:, half * 256:(half + 1) * 256]
                            rd.append(b_E[eb])
                        wr.append(BK.buf[ab])
                        mms.append(lambda e, lhs=lhs, rhs=rhs, ab=ab, first=(pr == 0 and half == 0), last=(pr == 3 and half == 1): e.matmul(
                            BK.t[ab][:, 0:256], lhs, rhs, start=first, stop=last))
                P.mm(mms, reads=rd, writes=wr)
                if pr == 3:
                    for q in range(2):
                        h = 2 * hp + q
                        ab = acc_bank[hp][q]
                        ob = q * 64
                        sbase = 64 - ob
                        r = q
                        if q == 0:
                            P.op("dve", lambda e, ab=ab, ob=ob, sbase=sbase, r=r: e.reciprocal(rcp[r][ob:ob + 64, :], BK.t[ab][sbase:sbase + 64, 0:256]),
                                 reads=[BK.buf[ab]], writes=[b_rcp[r]])
                        else:
                            P.op("act", lambda e, ab=ab, ob=ob, sbase=sbase, r=r: e.activation(rcp[r][ob:ob + 64, :], BK.t[ab][sbase:sbase + 64, 0:256], AF.Ln),
                                 reads=[BK.buf[ab]], writes=[b_rcp[r]])
                            P.op("act", lambda e, ob=ob, r=r: e.activation(rcp[r][ob:ob + 64, :], rcp[r][ob:ob + 64, :], AF.Exp, scale=-1.0),
                                 reads=[b_rcp[r]], writes=[b_rcp[r]])
                        P.op("dve", lambda e, ab=ab, ob=ob, r=r, h=h: e.tensor_tensor(
                            attnT[ob:ob + 64, h // 2, 256 * hg:256 * hg + 256], BK.t[ab][ob:ob + 64, 0:256], rcp[r][ob:ob + 64, :], ALU.mult),
                            reads=[BK.buf[ab], b_rcp[r]], writes=[b_attn[hg]])
                        BK.put(ab)

            LA = 2
            n = len(items)
            for i in range(min(LA, n)):
                issue_scores(i)
            for i in range(n):
                if i + LA < n:
                    issue_scores(i + LA)
                issue_pv(i)
                if side is not None:
                    next(side, None)
            if side is not None:
                for _ in side:
                    pass

        proj_chunk(0)
        norm_stats(xt, b_xt, 4)
        proj_chunk(1)
        for g in range(8):
            attn_halfgroup(2 * g, norm_stats_units(xt, b_xt, 4) if g + 2 < NCH else None)
            attn_halfgroup(2 * g + 1, proj_units(g + 2) if g + 2 < NCH else None)
        P.barrier()

    if DEBUG:
        with contextlib.ExitStack() as DB:
            tmp = sb(DB, "dbgtmp", [128, 4, TOK], F32)
            bt = Buf("dbgtmp")
            P.op("dve", lambda e: e.tensor_copy(tmp[:], attnT[:]), reads=b_attn, writes=[bt])
            P.dma("sp", "dbg", dbg["attnT"], tmp[:], reads=[bt])
            P.barrier()

    print("op counts", P.cnt, {k: v[1] for k, v in P.dma_sems.items()})
    W.close()
    if KSTOP in ("A1", "A1a", "A1b", "A1c"):
        A.close()
        ES.close()
        return nc
    b_xnew = Buf("xnew")
    with contextlib.ExitStack() as A2:
        winB = sb(A2, "winB", [128, 8, 1024], BF16)
        b_wB = Buf("winB")
        wout = sb(A2, "wout", [128, 8, D], BF16)
        b_wout = Buf("w_out")
        for kt in range(0, 8, 4):
            P.dma("pool", "win", winB[:, kt:kt + 4, :], win_v[:, kt:kt + 4, 1536:2560], writes=[b_wB])
        P.dma("pool", "wout", wout[:], w_out.rearrange("(kt p) n -> p kt n", p=128), writes=[b_wout])
        g1B = sb(A2, "g1B", [128, D], F32)
        b_g1 = Buf("g1B")
        P.dma("sp", "c0", g1B[:], gB_d[0], writes=[b_g1])
        dwk_s = sb(A2, "dwk_s", [128, 4, 31], F32)
        cvp_s = sb(A2, "cvp_s", [128, 3, 4], F32)
        b_cv = Buf("cv")
        P.dma("sp", "c0", dwk_s[:], dwk, writes=[b_cv])
        P.dma("sp", "c0", cvp_s[:], cvp, writes=[b_cv])
        HR = 3 * 512 + 272
        hring = sb(A2, "hring", [128, 4, HR], BF16)
        b_hslot = [Buf("h0"), Buf("h1"), Buf("h2"), Buf("hext")]
        diag = sb(A2, "diag", [128, 4, 31, 128], BF16)
        identb = sb(A2, "identb", [128, 128], BF16)
        vms = sb(A2, "vms", [128, 2, 512], BF16)
        sg = [sb(A2, "sg%d" % i, [128, 512], F32) for i in range(2)]
        b_sg = [Buf("sg0"), Buf("sg1")]
        ybuf = sb(A2, "ybuf", [128, 4, 512], F32)
        ysq = sb(A2, "ysq", [128, 4, 512], F32)
        b_y = Buf("y")
        b_ysq = Buf("ysq")
        mstat = sb(A2, "mstat", [128, 4, 512], F32)
        b_ms = Buf("mstat")
        convT = sb(A2, "convT", [128, 2, 4, 512], BF16)
        b_conv = [Buf("convT0"), Buf("convT1")]
        tt_ = [sb(A2, "pt%d" % i, [128, D], F32) for i in range(2)]
        b_tt = [Buf("pt0"), Buf("pt1")]
        xres = [sb(A2, "xres%d" % i, [128, D], F32) for i in range(2)]
        b_xres = [Buf("xres0"), Buf("xres1")]
        b_diag = Buf("diag")
        b_vm = Buf("vm")
        P.dma("pool", "c1", identb[:], ident_d, writes=[b_diag])
        P.dma("pool", "c1", vms[:], vm_d.rearrange("a p n -> p a n"), writes=[b_vm])
        norm_dma(xt, b_xt, xrows(0), 4, "xl")
        for ct in range(4):
            for tap in range(31):
                P.op("dve", lambda e, ct=ct, tap=tap: e.tensor_scalar(diag[:, ct, tap, :], identb[:], dwk_s[:, ct, tap:tap + 1], None, ALU.mult),
                     reads=[b_diag, b_cv], writes=[b_diag])
        norm_stats(xt, b_xt, 4)

        def glu_chunk(j):
            s = j % 3
            norm_tp(xt, b_xt, hxT, b_hx, 4, 0, lambda kt: SH1(kt, 0))
            if j + 1 < NCH:
                norm_dma(xt, b_xt, xrows(j + 1), 4, "xl")
            for ct in range(4):
                ba = BK.get()
                proj_fm(hxT, b_hx, winB, b_wB, ct * 128, 512, ba)
                bb = BK.get()
                proj_fm(hxT, b_hx, winB, b_wB, 512 + ct * 128, 512, bb)
                u = ct % 2
                evac(bb, "act", lambda e, bb=bb, u=u: e.activation(sg[u][:], BK.t[bb][:], AF.Sigmoid), writes=[b_sg[u]])
                BK.put(bb)
                evac(ba, "dve", lambda e, ba=ba, u=u, ct=ct: e.tensor_tensor(hring[:, ct, s * 512:(s + 1) * 512], BK.t[ba][:], sg[u][:], ALU.mult),
                     reads=[b_sg[u]], writes=[b_hslot[s]])
                BK.put(ba)
            if j in (0, NCH - 1):
                w = 0 if j == 0 else 1
                for ct in range(4):
                    P.op("pool", lambda e, w=w, ct=ct: e.tensor_tensor(hring[:, ct, s * 512:(s + 1) * 512], hring[:, ct, s * 512:(s + 1) * 512],
                                                                    vms[:, w, :], ALU.mult),
                         reads=[b_hslot[s], b_vm], writes=[b_hslot[s]])
            if s == 0:
                P.op("pool", lambda e: e.tensor_copy(hring[:, :, 1536:1536 + 272], hring[:, :, 0:272]), reads=[b_hslot[0]], writes=[b_hslot[3]])

        def conv_group(g):
            s = g % 3
            base = s * 512 + 241
            rd = [b_hslot[s], b_hslot[s + 1] if s < 2 else b_hslot[3], b_diag]
            for ct in range(4):
                bi = BK.get()
                P.mm([lambda e, tap=tap, ct=ct, bi=bi: e.matmul(BK.t[bi][:], diag[:, ct, tap, :], hring[:, ct, base + tap:base + tap + 512],
                                                               start=(tap == 0), stop=(tap == 30)) for tap in range(31)],
                     reads=rd, writes=[BK.buf[bi]])
                P.op("act", lambda e, ct=ct, bi=bi: e.activation(ybuf[:, ct, :], BK.t[bi][:], AF.Identity, bias=cvp_s[:, 0, ct:ct + 1]),
                     reads=[BK.buf[bi], b_cv], writes=[b_y])
                evac(bi, "act", lambda e, ct=ct, bi=bi: e.activation(ysq[:, ct, :], BK.t[bi][:], AF.Square, bias=cvp_s[:, 0, ct:ct + 1]),
                     reads=[b_cv], writes=[b_ysq])
                BK.put(bi)
            bs = BK.get()
            P.mm([lambda e, ct=ct: e.matmul(BK.t[bs][:], onesf[:], ybuf[:, ct, :], start=(ct == 0), stop=(ct == 3)) for ct in range(4)],
                 reads=[b_y, b_const], writes=[BK.buf[bs]])
            bq = BK.get()
            P.mm([lambda e, ct=ct: e.matmul(BK.t[bq][:], onesf[:], ysq[:, ct, :], start=(ct == 0), stop=(ct == 3)) for ct in range(4)],
                 reads=[b_ysq, b_const], writes=[BK.buf[bq]])
            P.op("act", lambda e: e.activation(mstat[:, 0, :], BK.t[bs][:], AF.Copy, scale=1.0 / 512), reads=[BK.buf[bs]], writes=[b_ms])
            evac(bs, "act", lambda e: e.activation(mstat[:, 1, :], BK.t[bs][:], AF.Square, scale=1.0 / 512), writes=[b_ms])
            BK.put(bs)
            evac(bq, "dve", lambda e: e.scalar_tensor_tensor(mstat[:, 2, :], BK.t[bq][:], 1.0 / 512, mstat[:, 1, :], ALU.mult, ALU.subtract),
                 reads=[b_ms], writes=[b_ms])
            BK.put(bq)
            P.op("act", lambda e: e.activation(mstat[:, 2, :], mstat[:, 2, :], AF.Ln, bias=EPS), reads=[b_ms], writes=[b_ms])
            P.op("act", lambda e: e.activation(mstat[:, 3, :], mstat[:, 2, :], AF.Exp, scale=-0.5), reads=[b_ms], writes=[b_ms])

        def conv_group_b(g):
            for ct in range(4):
                eng = "dve" if ct != 3 else "pool"
                P.op(eng, lambda e, ct=ct: e.tensor_tensor(ybuf[:, ct, :], ybuf[:, ct, :], mstat[:, 0, :], ALU.subtract),
                     reads=[b_y, b_ms], writes=[b_y])
                P.op(eng, lambda e, ct=ct: e.tensor_tensor(ybuf[:, ct, :], ybuf[:, ct, :], mstat[:, 3, :], ALU.mult),
                     reads=[b_y, b_ms], writes=[b_y])
                P.op("act", lambda e, ct=ct: e.activation(convT[:, g % 2, ct, :], ybuf[:, ct, :], AF.Silu, scale=cvp_s[:, 1, ct:ct + 1],
                                                          bias=cvp_s[:, 2, ct:ct + 1]),
                     reads=[b_y, b_cv], writes=[b_conv[g % 2]])
            if DEBUG and g == 0:
                P.dma("sp", "dbg", dbg["mstat"], mstat[:], reads=[b_ms])
                P.dma("sp", "dbg", dbg["ysq"], ysq[:], reads=[b_ysq])
                P.dma("sp", "dbg", dbg["hring"], hring[:], reads=b_hslot)
                P.barrier()
                P.op("dve", lambda e: e.tensor_copy(ysq[:], convT[:, 0]), reads=[b_conv[0]], writes=[b_ysq])
                P.dma("sp", "dbg", dbg["convT"], ysq[:], reads=[b_ysq])

        def outproj_group(g):
            for tt in range(4):
                u = tt % 2
                tok0 = 512 * g + tt * 128
                P.dma("sp", "xres%d" % u, xres[u][:], xe[HALO + tok0:HALO + tok0 + 128, :], writes=[b_xres[u]])
                for nh in range(2):
                    bi = BK.get()
                    mms = []
                    for kt in range(8):
                        if kt < 4:
                            lhs = attnT[:, kt, tok0:tok0 + 128]
                        else:
                            lhs = convT[:, g % 2, kt - 4, tt * 128:(tt + 1) * 128]
                        mms.append(lambda e, lhs=lhs, kt=kt, nh=nh, bi=bi: e.matmul(BK.t[bi][:], lhs, wout[:, kt, nh * 512:(nh + 1) * 512],
                                                                                 start=(kt == 0), stop=(kt == 7)))
                    P.mm(mms, reads=[b_conv[g % 2], b_wout] + b_attn, writes=[BK.buf[bi]])
                    evac(bi, "dve", lambda e, bi=bi, nh=nh, u=u: e.tensor_tensor(tt_[u][:, nh * 512:(nh + 1) * 512], BK.t[bi][:],
                                                                              g1B[:, nh * 512:(nh + 1) * 512], ALU.mult),
                         reads=[b_g1], writes=[b_tt[u]])
                    BK.put(bi)
                P.op("pool", lambda e, u=u: e.tensor_tensor(xres[u][:], tt_[u][:], xres[u][:], ALU.add),
                     reads=[b_tt[u], b_xres[u]], writes=[b_xres[u]])
                P.dma("sp", "xst%d" % u, xnew_d[tok0:tok0 + 128, :], xres[u][:], reads=[b_xres[u]], writes=[b_xnew])
                if DEBUG:
                    P.dma("sp", "dbg", dbg["xnew"][tok0:tok0 + 128, :], xres[u][:], reads=[b_xres[u]])

        glu_chunk(0)
        norm_stats(xt, b_xt, 4)
        glu_chunk(1)
        norm_stats(xt, b_xt, 4)
        for g in range(8):
            conv_group(g)
            if g + 2 < NCH:
                glu_chunk(g + 2)
            conv_group_b(g)
            if g >= 1:
                outproj_group(g - 1)
            if g + 3 < NCH:
                norm_stats(xt, b_xt, 4)
        outproj_group(7)
        P.barrier()
    A.close()
    if KSTOP == "A2":
        ES.close()
        return nc

    with contextlib.ExitStack() as B:
        CB = 256
        NCB = TOK // CB
        w1s = sb(B, "w1s", [128, 8, 4 * D], BF16)
        w2s = sb(B, "w2s", [128, 32, D], BF16)
        b_w1 = Buf("w1")
        b_w2 = Buf("w2")
        w1_v = w1.rearrange("(kt p) n -> p kt n", p=128)
        w2_v = w2.rearrange("(kt p) n -> p kt n", p=128)
        for kt in range(0, 8, 2):
            P.dma("pool", "w1", w1s[:, kt:kt + 2, :], w1_v[:, kt:kt + 2, :], writes=[b_w1])
        for kt in range(0, 32, 8):
            P.dma("pool", "w2", w2s[:, kt:kt + 8, :], w2_v[:, kt:kt + 8, :], writes=[b_w2])
        xtB = sb(B, "xtB", [128, 2, D], F32)
        b_xtB = Buf("xtB")
        hx2T = sb(B, "hx2T", [128, 8, CB], BF16)
        b_hx2 = Buf("hx2")
        hT = sb(B, "hT", [128, 32, CB], BF16)
        b_hT = Buf("hT")
        fgB = sb(B, "fgBs", [128, D], F32)
        g2B = sb(B, "g2B", [128, D], F32)
        b_fg = Buf("fg")
        sq = [sb(B, "sq%d" % i, [128, CB], F32) for i in range(2)]
        b_sq = [Buf("sq0"), Buf("sq1")]
        xr = [sb(B, "xr%d" % i, [128, D], F32) for i in range(2)]
        b_xr = [Buf("xr0"), Buf("xr1")]
        pt = [sb(B, "ptB%d" % i, [128, D], F32) for i in range(2)]
        b_pt = [Buf("ptB0"), Buf("ptB1")]
        ot = [sb(B, "ot%d" % i, [128, D], F32) for i in range(2)]
        b_ot = [Buf("ot0"), Buf("ot1")]
        fst = sb(B, "fst", [128, 8], F32)
        b_fst = Buf("fst")
        P.dma("sp", "c0", fgB[:], fgB_d, writes=[b_fg])
        P.dma("sp", "c0", g2B[:], gB_d[1], writes=[b_fg])
        b_outd = Buf("outd")
        nrows = lambda c: (lambda t: xnew_d[c * CB + t * 128:c * CB + (t + 1) * 128, :])

        def mlp1(c):
            for ft in range(32):
                bi = BK.get()
                P.mm([lambda e, kt=kt, ft=ft, bi=bi: e.matmul(BK.t[bi][:, 0:CB], w1s[:, kt, ft * 128:(ft + 1) * 128], hx2T[:, kt, :],
                                                            start=(kt == 0), stop=(kt == 7)) for kt in range(8)],
                     reads=[b_hx2, b_w1], writes=[BK.buf[bi]])
                u = ft % 2
                P.op("act", lambda e, bi=bi, u=u: e.activation(sq[u][:], BK.t[bi][:, 0:CB], AF.Square), reads=[BK.buf[bi]], writes=[b_sq[u]])
                evac(bi, "dve", lambda e, bi=bi, u=u, ft=ft: e.scalar_tensor_tensor(hT[:, ft, :], BK.t[bi][:, 0:CB], 0.0, sq[u][:], ALU.is_gt, ALU.mult),
                     reads=[b_sq[u]], writes=[b_hT])
                BK.put(bi)

        def mlp2(c):
            for tt in range(CB // 128):
                u = tt % 2
                tok0 = c * CB + tt * 128
                P.dma("sp", "xr%d" % u, xr[u][:], xnew_d[tok0:tok0 + 128, :], writes=[b_xr[u]])
                for nh in range(2):
                    bi = BK.get()
                    P.mm([lambda e, ft=ft, tt=tt, nh=nh, bi=bi: e.matmul(BK.t[bi][:], hT[:, ft, tt * 128:(tt + 1) * 128],
                                                                     w2s[:, ft, nh * 512:(nh + 1) * 512], start=(ft == 0), stop=(ft == 31))
                          for ft in range(32)], reads=[b_hT, b_w2], writes=[BK.buf[bi]])
                    evac(bi, "dve", lambda e, bi=bi, nh=nh, u=u: e.tensor_tensor(pt[u][:, nh * 512:(nh + 1) * 512], BK.t[bi][:],
                                                                              g2B[:, nh * 512:(nh + 1) * 512], ALU.mult),
                         reads=[b_fg], writes=[b_pt[u]])
                    BK.put(bi)
                P.op("pool", lambda e, u=u: e.tensor_tensor(pt[u][:], pt[u][:], xr[u][:], ALU.add),
                     reads=[b_pt[u], b_xr[u]], writes=[b_pt[u]])
                P.op("act", lambda e, u=u: e.activation(junk[:], pt[u][:], AF.Square, accum_out=fst[:, u:u + 1]),
                     reads=[b_pt[u], b_junk], writes=[b_fst, b_junk])
                P.op("act", lambda e, u=u: e.activation(fst[:, 2 + u:3 + u], fst[:, u:u + 1], AF.Ln, scale=1.0 / D, bias=EPS),
                     reads=[b_fst], writes=[b_fst])
                P.op("act", lambda e, u=u: e.activation(fst[:, 4 + u:5 + u], fst[:, 2 + u:3 + u], AF.Exp, scale=-0.5), reads=[b_fst], writes=[b_fst])
                P.op("dve", lambda e, u=u: e.scalar_tensor_tensor(ot[u][:], pt[u][:], fst[:, 4 + u:5 + u], fgB[:], ALU.mult, ALU.mult),
                     reads=[b_pt[u], b_fst, b_fg], writes=[b_ot[u]])
                P.dma("sp", "ost%d" % u, out_d[tok0:tok0 + 128, :], ot[u][:], reads=[b_ot[u]], writes=[b_outd])

        norm_dma(xtB, b_xtB, nrows(0), 2, "xlB")
        norm_stats(xtB, b_xtB, 2)
        norm_tp(xtB, b_xtB, hx2T, b_hx2, 2, 2, lambda kt: SH2(kt))
        norm_dma(xtB, b_xtB, nrows(1), 2, "xlB")
        for c in range(NCB):
            if c + 1 < NCB:
                norm_stats(xtB, b_xtB, 2)
            mlp1(c)
            if c + 1 < NCB:
                norm_tp(xtB, b_xtB, hx2T, b_hx2, 2, 2, lambda kt: SH2(kt))
                if c + 2 < NCB:
                    norm_dma(xtB, b_xtB, nrows(c + 2), 2, "xlB")
            mlp2(c)
        P.barrier()
    ES.close()
    print("op counts", P.cnt, {k: v[1] for k, v in P.dma_sems.items()})
    return nc


def _consts(half):
    row0 = half * 64
    ti = np.arange(TE)
    gt = half * TOK + ti - HALO
    grow = np.floor_divide(gt, 64).astype(np.float32)
    gcol = np.mod(gt, 64).astype(np.float32)
    inv = (np.float32(10000.0) ** (-np.arange(16, dtype=np.float32) / np.float32(16))).astype(np.float32)
    rope = np.zeros((128, 2, TE), np.float32)
    for p in range(128):
        d = p % 64
        pos = grow if d < 32 else gcol
        ang = (pos * inv[d % 16]).astype(np.float32)
        rope[p, 0] = np.cos(ang)
        sn = np.sin(ang)
        rope[p, 1] = -sn if (d % 32) < 16 else sn
    rope = np.ascontiguousarray(rope.reshape(128, 2, NCH, 512).transpose(2, 0, 1, 3))
    perm = np.zeros((128, 128), np.float32)
    for m in range(128):
        k = m + 16 if (m % 32) < 16 else m - 16
        perm[k, m] = 1.0
    khot = np.zeros((12, 8, 3, 512), np.float32)
    for s in range(3):
        for i in range(512):
            khot[(8 * s - 4 + i // 64) % 12, :, s, i] = 1.0
    khot = khot.reshape(12, 8, 1536)
    bmask = np.zeros((NCH, 12, 8, 512), np.float32)
    for j in range(NCH):
        for rr in range(8):
            tau = -HALO + 512 * j + 64 * rr
            if tau < 0 or tau >= TOK:
                continue
            lr = tau // 64
            r0 = 4 * (lr // 4)
            gr = row0 + lr
            rs = min(max(gr - 4, 0), 120)
            for kr in range(r0 - 4, r0 + 8):
                gk = row0 + kr
                ok = rs <= gk < rs + 8
                bmask[j, kr % 12, :, rr * 64:(rr + 1) * 64] = 0.0 if ok else NEG
    vm = np.ones((2, 128, 512), np.float32)
    g0 = half * TOK - HALO + np.arange(512)
    vm[0, :, :] = (g0 >= 0).astype(np.float32)[None, :]
    g8 = half * TOK - HALO + 512 * 8 + np.arange(512)
    vm[1, :, :] = (g8 < SEQ).astype(np.float32)[None, :]
    return rope, perm, khot, bmask, vm


def _mtab(rpb):
    tab = np.full((128, 8, 6, 4, 64), NEG, np.float32)
    qc = np.arange(64)
    cs = np.clip(qc - 8, 0, 48)
    for t in range(6):
        for i in range(2):
            for rq in range(4):
                dl = -4 + 2 * t + i - rq
                if dl < -7 or dl > 7:
                    continue
                for kc in range(64):
                    ok = (kc >= cs) & (kc < cs + 16)
                    co = np.clip(kc - qc + 15, 0, 30)
                    vals = rpb[:, dl + 7, :][:, co]
                    tab[i * 64 + kc, :, t, rq, :] = np.where(ok[None, :], vals, NEG)
    return tab.reshape(128, 8, 1536)


_CACHE = {}


def kernel(x, c, ctx, c_ctx, w_ada, b_ada, norm1_g, norm2_g, w_in, dw_kernel, dw_bias,
           conv_ln_g, conv_ln_b, rpb, w_out, w_mlp1, w_mlp2, final_g):
    f = lambda a: np.ascontiguousarray(np.asarray(a, dtype=np.float32))
    x, c, ctx, c_ctx = f(x), f(c), f(ctx), f(c_ctx)
    w_ada0, b_ada0 = f(w_ada)[0], f(b_ada)[0]
    fm = lambda v, n: np.ascontiguousarray(v.reshape(n, 128).T)
    shared = {
        "w_ada": w_ada0,
        "b_adaT": fm(b_ada0, 48),
        "b_ada_g": np.ascontiguousarray(np.concatenate([b_ada0[2 * D:3 * D], b_ada0[5 * D:6 * D]])[None, :]),
        "n1gT": fm(f(norm1_g)[0], 8),
        "n2gT": fm(f(norm2_g)[0], 8),
        "fgB": np.ascontiguousarray(np.broadcast_to(f(final_g)[None, :], (128, D))),
        "w_in": f(w_in)[0],
        "w_out": f(w_out)[0],
        "w1": f(w_mlp1)[0],
        "w2": f(w_mlp2)[0],
        "dwk": np.ascontiguousarray(f(dw_kernel)[0].reshape(31, 4, 128).transpose(2, 1, 0)),
        "cvp": np.ascontiguousarray(np.stack([fm(f(dw_bias)[0], 4), fm(f(conv_ln_g)[0], 4), fm(f(conv_ln_b)[0], 4)], axis=1)),
        "ident": np.eye(128, dtype=np.float32),
        "mtab": _mtab(f(rpb)[0]),
    }
    in_maps = []
    for core in range(8):
        b, half = core // 2, core % 2
        rope, perm, khot, bmask, vm = _consts(half)
        xe = np.zeros((TE, D), np.float32)
        lo = half * TOK - HALO
        a0, a1 = max(lo, 0), min(lo + TE, SEQ)
        xe[a0 - lo:a1 - lo] = x[b, a0:a1]
        cv = np.stack([fm(c[b], 8), fm(c_ctx, 8)], axis=2)
        m = dict(shared)
        m.update({"xe": xe, "ctxb": np.ascontiguousarray(ctx[b]), "cvec": np.ascontiguousarray(cv), "perm": perm,
                  "rope": rope, "khot": khot, "bmask": bmask, "vm": vm})
        in_maps.append(m)
    if "nc" not in _CACHE:
        _CACHE["nc"] = build_nc()
    ncore = int(os.environ.get("KCORES", "8"))
    res = run_bass_kernel_spmd(_CACHE["nc"], in_maps[:ncore], core_ids=list(range(ncore)))
    out = np.zeros((NB, SEQ, D), np.float32)
    for core in range(ncore):
        b, half = core // 2, core % 2
        out[b, half * TOK:(half + 1) * TOK] = res.results[core]["out"]
    if DEBUG:
        kernel.dbg = res.results
    return out
```
